# Optimizing a Trainium2 kernel written in Bass

```python
import math
import jax, jax.numpy as jnp
from jax import lax
import numpy as np

D_MODEL = 4096
BATCH = 8
SEQ = 2048
DEPTH = 2
DEC_BATCH = 16
DEC_SEQ = 16
PAST_LEN = 1024

CHUNK = 64
W_A = D_MODEL
S5_GROUP = 16
N_GROUPS_A = W_A // S5_GROUP
P_STATE = 64
W_B = D_MODEL
N_BLOCKS_B = 16
BLOCK_B = W_B // N_BLOCKS_B
CONV_W = 4
LRU_C = 8.0
D_FF = -(-8 * D_MODEL // (3 * 256)) * 256
IN_COLS = W_A + 2 * W_B + 2 * D_MODEL
EPS = 1e-6

kernel_name = 'hybrid_s5_rglru_stream_step'


def rmsnorm(x, g):
    xf = x.astype(jnp.float32)
    y = xf * lax.rsqrt(jnp.mean(xf * xf, axis=-1, keepdims=True) + EPS)
    return (y * g.astype(jnp.float32)).astype(x.dtype)


def s5_scan(u, h0_re, h0_im, lam_re, lam_im, log_step, b_re, b_im, c_re, c_im, d_skip):
    f32 = jnp.float32
    n, l, _ = u.shape
    blk = min(CHUNK, l)
    nblk = l // blk
    lam = lax.complex(lam_re.astype(f32), lam_im.astype(f32))
    step = jnp.exp(log_step.astype(f32))[:, None]
    a_bar = jnp.exp(lam * step)
    bmat = lax.complex(b_re.astype(f32), b_im.astype(f32))
    b_bar = ((a_bar - 1.0) / lam)[..., None] * bmat
    cmat = lax.complex(c_re.astype(f32), c_im.astype(f32))
    t_idx = jnp.arange(1, blk + 1, dtype=f32)
    a_pow = jnp.exp(lam[None] * (step[None] * t_idx[:, None, None]))
    ub_all = u.astype(f32).reshape(n, nblk, blk, N_GROUPS_A, S5_GROUP).transpose(1, 0, 2, 3, 4)

    def combine(e1, e2):
        a1, b1 = e1
        a2, b2 = e2
        return a1 * a2, a2 * b1 + b2

    def block_step(h, ub):
        bu = jnp.einsum('nlgh,gph->nlgp', ub.astype(jnp.complex64), b_bar)
        a = jnp.broadcast_to(a_bar, bu.shape)
        _, hs = lax.associative_scan(combine, (a, bu), axis=1)
        hs = hs + a_pow[None] * h[:, None]
        y = jnp.einsum('nlgp,ghp->nlgh', hs, cmat).real
        return hs[:, -1], y

    h0 = lax.complex(h0_re.astype(f32), h0_im.astype(f32))
    h_last, ys = lax.scan(block_step, h0, ub_all)
    y = ys.transpose(1, 0, 2, 3, 4).reshape(n, l, W_A) + d_skip.astype(f32) * u.astype(f32)
    return y, h_last.real, h_last.imag


def rglru_branch(xb, conv_buf, h0, conv_w, conv_b, w_r, b_r, w_i, b_i, lru_lam):
    f32 = jnp.float32
    n, l, _ = xb.shape
    xp = jnp.concatenate([conv_buf.astype(f32), xb.astype(f32)], axis=1)
    cw = conv_w.astype(f32)
    xc = conv_b.astype(f32) + sum(cw[k] * xp[:, k:k + l] for k in range(CONV_W))
    new_buf = xp[:, -(CONV_W - 1):]
    xr = xc.reshape(n, l, N_BLOCKS_B, BLOCK_B)
    r = jax.nn.sigmoid(jnp.einsum('nlkc,kcd->nlkd', xr, w_r.astype(f32)) + b_r.astype(f32).reshape(N_BLOCKS_B, BLOCK_B))
    i = jax.nn.sigmoid(jnp.einsum('nlkc,kcd->nlkd', xr, w_i.astype(f32)) + b_i.astype(f32).reshape(N_BLOCKS_B, BLOCK_B))
    r = r.reshape(n, l, W_B)
    i = i.reshape(n, l, W_B)
    log_a = -LRU_C * r * jax.nn.softplus(-lru_lam.astype(f32))
    a = jnp.exp(log_a)
    bx = jnp.sqrt(-jnp.expm1(2.0 * log_a)) * (i * xc)

    def step(h, ab):
        a_t, b_t = ab
        h = a_t * h + b_t
        return h, h

    h_last, hs = lax.scan(step, h0.astype(f32), (a.swapaxes(0, 1), bx.swapaxes(0, 1)))
    return hs.swapaxes(0, 1), h_last, new_buf


def layer(x, s5_re, s5_im, lru_h, conv_buf, g_pre_mix, w_in, lam_re, lam_im, log_step, b_re, b_im,
          c_re, c_im, d_skip, w_glu, b_glu, conv_w, conv_b, w_r, b_r, w_i, b_i, lru_lam, p_a, p_b,
          w_out, g_post_mix, g_pre_ffn, w_gate, w_up, w_down, g_post_ffn):
    dt = x.dtype
    h = rmsnorm(x, g_pre_mix)
    z = h @ w_in
    u_a, x_b, gate_b, g_a, g_b = jnp.split(
        z, [W_A, W_A + W_B, W_A + 2 * W_B, W_A + 2 * W_B + D_MODEL], axis=-1)
    y_a, s5_re_new, s5_im_new = s5_scan(u_a, s5_re, s5_im, lam_re, lam_im, log_step,
                                        b_re, b_im, c_re, c_im, d_skip)
    y_a = jax.nn.gelu(y_a).astype(dt)
    y_a = y_a * jax.nn.sigmoid(y_a @ w_glu + b_glu)
    y_b, lru_new, conv_new = rglru_branch(x_b, conv_buf, lru_h, conv_w, conv_b, w_r, b_r, w_i, b_i, lru_lam)
    y_b = y_b.astype(dt) * jax.nn.gelu(gate_b)
    merged = jax.nn.sigmoid(g_a) * (y_a @ p_a) + jax.nn.sigmoid(g_b) * (y_b @ p_b)
    x = x + rmsnorm(merged @ w_out, g_post_mix)
    h = rmsnorm(x, g_pre_ffn)
    f = (jax.nn.silu(h @ w_gate) * (h @ w_up)) @ w_down
    x = x + rmsnorm(f, g_post_ffn)
    return x, s5_re_new, s5_im_new, lru_new, conv_new


def setup_inputs(seed: int = 0) -> dict:
    key = jax.random.key(seed)
    ks = iter(jax.random.split(key, 40))
    f32 = jnp.float32

    def nrm(shape, scale):
        return jax.random.normal(next(ks), shape, f32) * scale

    def gain():
        return 1.0 + nrm((DEPTH, D_MODEL), 0.02)

    x_prompt = nrm((BATCH, SEQ, D_MODEL), 1.0)
    x_sample = nrm((DEC_BATCH, DEC_SEQ, D_MODEL), 1.0)
    state_s5_re = nrm((DEPTH, DEC_BATCH, N_GROUPS_A, P_STATE), 0.5)
    state_s5_im = nrm((DEPTH, DEC_BATCH, N_GROUPS_A, P_STATE), 0.5)
    state_lru = nrm((DEPTH, DEC_BATCH, W_B), 0.5)
    cache_conv = nrm((DEPTH, DEC_BATCH, CONV_W - 1, W_B), 1.0)

    g_pre_mix = gain()
    w_in = nrm((DEPTH, D_MODEL, IN_COLS), D_MODEL ** -0.5)
    s5_lam_re = -0.5 + nrm((DEPTH, N_GROUPS_A, P_STATE), 0.01)
    s5_lam_im = math.pi * jnp.arange(P_STATE, dtype=f32) + nrm((DEPTH, N_GROUPS_A, P_STATE), 0.01)
    s5_log_step = jax.random.uniform(next(ks), (DEPTH, N_GROUPS_A), f32, math.log(1e-3), math.log(1e-1))
    s5_b_re = nrm((DEPTH, N_GROUPS_A, P_STATE, S5_GROUP), (2 * S5_GROUP) ** -0.5)
    s5_b_im = nrm((DEPTH, N_GROUPS_A, P_STATE, S5_GROUP), (2 * S5_GROUP) ** -0.5)
    s5_c_re = nrm((DEPTH, N_GROUPS_A, S5_GROUP, P_STATE), (2 * P_STATE) ** -0.5)
    s5_c_im = nrm((DEPTH, N_GROUPS_A, S5_GROUP, P_STATE), (2 * P_STATE) ** -0.5)
    s5_d = nrm((DEPTH, W_A), 1.0)
    w_glu = nrm((DEPTH, W_A, W_A), W_A ** -0.5)
    b_glu = nrm((DEPTH, W_A), 0.01)
    conv_w = nrm((DEPTH, CONV_W, W_B), CONV_W ** -0.5)
    conv_b = nrm((DEPTH, W_B), 0.01)
    lru_w_r = nrm((DEPTH, N_BLOCKS_B, BLOCK_B, BLOCK_B), BLOCK_B ** -0.5)
    lru_b_r = nrm((DEPTH, W_B), 0.01)
    lru_w_i = nrm((DEPTH, N_BLOCKS_B, BLOCK_B, BLOCK_B), BLOCK_B ** -0.5)
    lru_b_i = nrm((DEPTH, W_B), 0.01)
    a_target = jax.random.uniform(next(ks), (DEPTH, W_B), f32, 0.9, 0.999)
    v = a_target ** (1.0 / LRU_C)
    lru_lam = jnp.log(v) - jnp.log1p(-v)
    p_a = nrm((DEPTH, W_A, D_MODEL), W_A ** -0.5)
    p_b = nrm((DEPTH, W_B, D_MODEL), W_B ** -0.5)
    w_out = nrm((DEPTH, D_MODEL, D_MODEL), D_MODEL ** -0.5)
    g_post_mix = gain()
    g_pre_ffn = gain()
    w_gate = nrm((DEPTH, D_MODEL, D_FF), D_MODEL ** -0.5)
    w_up = nrm((DEPTH, D_MODEL, D_FF), D_MODEL ** -0.5)
    w_down = nrm((DEPTH, D_FF, D_MODEL), D_FF ** -0.5)
    g_post_ffn = gain()
    return {'x_prompt': x_prompt, 'x_sample': x_sample, 'state_s5_re': state_s5_re,
            'state_s5_im': state_s5_im, 'state_lru': state_lru, 'cache_conv': cache_conv,
            'g_pre_mix': g_pre_mix, 'w_in': w_in, 's5_lam_re': s5_lam_re, 's5_lam_im': s5_lam_im,
            's5_log_step': s5_log_step, 's5_b_re': s5_b_re, 's5_b_im': s5_b_im, 's5_c_re': s5_c_re,
            's5_c_im': s5_c_im, 's5_d': s5_d, 'w_glu': w_glu, 'b_glu': b_glu, 'conv_w': conv_w,
            'conv_b': conv_b, 'lru_w_r': lru_w_r, 'lru_b_r': lru_b_r, 'lru_w_i': lru_w_i,
            'lru_b_i': lru_b_i, 'lru_lam': lru_lam, 'p_a': p_a, 'p_b': p_b, 'w_out': w_out,
            'g_post_mix': g_post_mix, 'g_pre_ffn': g_pre_ffn, 'w_gate': w_gate, 'w_up': w_up,
            'w_down': w_down, 'g_post_ffn': g_post_ffn}


def reference(x_prompt, x_sample, state_s5_re, state_s5_im, state_lru, cache_conv,
              g_pre_mix, w_in, s5_lam_re, s5_lam_im, s5_log_step, s5_b_re, s5_b_im, s5_c_re,
              s5_c_im, s5_d, w_glu, b_glu, conv_w, conv_b, lru_w_r, lru_b_r, lru_w_i, lru_b_i,
              lru_lam, p_a, p_b, w_out, g_post_mix, g_pre_ffn, w_gate, w_up, w_down, g_post_ffn):
    nb = x_prompt.shape[0]
    f32 = jnp.float32
    xp, xs = x_prompt, x_sample
    p_re, p_im, p_lru, p_conv = [], [], [], []
    s_re, s_im, s_lru, s_conv = [], [], [], []
    for l in range(DEPTH):
        lp = (g_pre_mix[l], w_in[l], s5_lam_re[l], s5_lam_im[l], s5_log_step[l], s5_b_re[l],
              s5_b_im[l], s5_c_re[l], s5_c_im[l], s5_d[l], w_glu[l], b_glu[l], conv_w[l], conv_b[l],
              lru_w_r[l], lru_b_r[l], lru_w_i[l], lru_b_i[l], lru_lam[l], p_a[l], p_b[l], w_out[l],
              g_post_mix[l], g_pre_ffn[l], w_gate[l], w_up[l], w_down[l], g_post_ffn[l])
        xp, a_re, a_im, a_lru, a_conv = layer(
            xp, jnp.zeros((nb, N_GROUPS_A, P_STATE), f32), jnp.zeros((nb, N_GROUPS_A, P_STATE), f32),
            jnp.zeros((nb, W_B), f32), jnp.zeros((nb, CONV_W - 1, W_B), f32), *lp)
        xs, b_re_, b_im_, b_lru, b_conv = layer(
            xs, state_s5_re[l], state_s5_im[l], state_lru[l], cache_conv[l], *lp)
        p_re.append(a_re); p_im.append(a_im); p_lru.append(a_lru); p_conv.append(a_conv)
        s_re.append(b_re_); s_im.append(b_im_); s_lru.append(b_lru); s_conv.append(b_conv)
    sd = state_lru.dtype
    new_s5_re_prompt = jnp.stack(p_re).astype(sd)
    new_s5_im_prompt = jnp.stack(p_im).astype(sd)
    new_lru_prompt = jnp.stack(p_lru).astype(sd)
    new_conv_prompt = jnp.stack(p_conv).astype(sd)
    new_s5_re_sample = jnp.stack(s_re).astype(sd)
    new_s5_im_sample = jnp.stack(s_im).astype(sd)
    new_lru_sample = jnp.stack(s_lru).astype(sd)
    new_conv_sample = jnp.stack(s_conv).astype(sd)
    return (xp, xs, new_s5_re_prompt, new_s5_im_prompt, new_lru_prompt, new_conv_prompt,
            new_s5_re_sample, new_s5_im_sample, new_lru_sample, new_conv_sample)
```

```python
import math
from contextlib import ExitStack
import numpy as np
import concourse.bass as bass
import concourse.mybir as mybir
from concourse.bass_utils import run_bass_kernel_spmd

F32 = mybir.dt.float32
BF16 = mybir.dt.bfloat16
AF = mybir.ActivationFunctionType
ALU = mybir.AluOpType
EPOCH = 16000
NDSEM = 28
P_STATE = 64
HG = 16
T = 4
EPS = 1e-6


class Cfg:
    def __init__(s, D=4096, SEQ=2048, DEC_SEQ=16, NS=2, DEPTH=2, NP=256, DFF=11008, KC=16, NSLOT=3, QJ=8):
        s.D = D; s.SEQ = SEQ; s.DEC_SEQ = DEC_SEQ; s.NS = NS; s.DEPTH = DEPTH; s.NP = NP; s.DFF = DFF
        s.KT = D // 128; s.FT = DFF // 128; s.M2 = D // 256; s.F2 = DFF // 256
        s.G = D // HG; s.GQ = s.G // 2; s.NB = D // 256
        s.KC = min(KC, s.KT); s.NSLOT = NSLOT
        s.NT = SEQ // NP; s.NSAMP = NS * DEC_SEQ; s.NMAX = NP + s.NSAMP
        s.QJ = min(QJ, s.KT); s.NQ = s.KT // s.QJ
        s.NCHMAX = s.NMAX // T
        assert SEQ % NP == 0 and NP % 128 == 0 and DEC_SEQ % T == 0 and s.NSAMP <= 128
        assert DFF % 256 == 0 and D % 256 == 0


class Buf:
    __slots__ = ("name", "w", "r", "region", "wd")

    def __init__(self, name="", region=False):
        self.name = name; self.w = None; self.r = {}; self.region = region; self.wd = {}


class Sched:
    ENG = ["pe", "act", "dve", "pool", "sp"]

    def __init__(self):
        self.ops = {e: [] for e in self.ENG}
        self.cnt = {e: 0 for e in self.ENG}
        self.known = {e: {} for e in self.ENG}
        self.dma_tot = [0] * NDSEM
        self.dma_rr = 0
        self.arena_dma = []

    def _need(self, eng, key, val):
        k = self.known[eng]
        if k.get(key, 0) >= val:
            return
        k[key] = val
        self.ops[eng].append(("wait", key, val))

    def _dep(self, eng, tok, war=False):
        key, val, src = tok
        if src == eng and eng == "pe":
            return
        self._need(eng, key, val)

    def _deps(self, eng, reads, writes):
        for b in reads:
            if b.w is not None:
                self._dep(eng, b.w)
            for t in b.wd.values():
                self._dep(eng, t)
        for b in writes:
            if b.region:
                continue
            if b.w is not None:
                self._dep(eng, b.w)
            for t in b.r.values():
                self._dep(eng, t, war=True)

    def op(self, eng, method, reads=(), writes=(), inc=True, **kw):
        self._deps(eng, reads, writes)
        idx = self.cnt[eng]
        key = (eng, idx // EPOCH); val = idx % EPOCH + 1
        if inc:
            self.cnt[eng] += 1
        self.ops[eng].append(("op", method, kw, key if inc else None))
        tok = (key, val, eng)
        for b in reads:
            b.r[eng] = tok
        for b in writes:
            b.w = tok; b.r = {}
        return tok

    def dma(self, eng, out, in_, reads=(), writes=(), arena=False, **kw):
        s = self.dma_rr; self.dma_rr = (s + 1) % NDSEM
        if self.dma_tot[s] > 0:
            self._need(eng, ("d", s), self.dma_tot[s])
        if arena:
            for (key, val) in getattr(self, "bar_toks", []):
                self._need(eng, key, val)
        self._deps(eng, reads, writes)
        self.dma_tot[s] += 16
        self.ops[eng].append(("dma", out, in_, kw, s))
        tok = (("d", s), self.dma_tot[s], "dma")
        for b in reads:
            b.r[("d", s)] = tok
        for b in writes:
            if b.region:
                b.wd[("d", s)] = tok
            else:
                b.w = tok; b.r = {}
        if arena:
            self.arena_dma.append(tok)
        return tok

    def barrier(self):
        ce = ["pe", "act", "dve", "pool"]
        toks = []
        for e in ce:
            if self.cnt[e] > 0:
                idx = self.cnt[e] - 1
                toks.append((e, ((e, idx // EPOCH), idx % EPOCH + 1)))
        for e in ce:
            for (src, (key, val)) in toks:
                if src != e:
                    self._need(e, key, val)
            for (key, val, _) in self.arena_dma:
                self._need(e, key, val)
        self.bar_toks = [kv for (_, kv) in toks] + [(key, val) for (key, val, _) in self.arena_dma]
        self.arena_dma = []

    def finish(self):
        for s in range(NDSEM):
            if self.dma_tot[s] > 0:
                self._need("sp", ("d", s), self.dma_tot[s])
        ce = ["pe", "act", "dve", "pool"]
        for e in ce:
            if self.cnt[e] > 0:
                idx = self.cnt[e] - 1
                self._need("sp", (e, idx // EPOCH), idx % EPOCH + 1)

    def sem_keys(self):
        keys = set()
        for e in self.ENG:
            for o in self.ops[e]:
                if o[0] == "wait":
                    keys.add(o[1])
                elif o[0] == "op" and o[3] is not None:
                    keys.add(o[3])
                elif o[0] == "dma":
                    keys.add(("d", o[4]))
        return sorted(keys, key=str)


def dram_ap(t, offset, ap):
    return bass.AP(t.tensor, offset, ap)


def wregions(cfg):
    regs = {}
    sizes = {}

    def add(key, nk, count):
        off = sizes.get(key[0], 0)
        regs[key] = (off, nk, count)
        sizes[key[0]] = off + count * 128 * nk * 256

    D, DFF, KC = cfg.D, cfg.DFF, cfg.KC
    for name, K, M in [("win", D, 5 * D), ("glu", D, D), ("pa", D, D), ("pb", D, D), ("wout", D, D),
                       ("gate", D, DFF), ("up", D, DFF), ("down", DFF, D)]:
        kts = K // 128
        for kc in range((kts + KC - 1) // KC):
            add((name, kc), min(KC, kts - kc * KC), M // 256)
    add(("wr", 0), 2, cfg.NB); add(("wi", 0), 2, cfg.NB)
    add(("wx", 0), 16, cfg.KT); add(("wy", 0), 16, cfg.KT); add(("wk", 0), 2, cfg.KT)
    return regs, sizes


def weight_order(cfg):
    o = []
    nkc = (cfg.KT + cfg.KC - 1) // cfg.KC
    fkc = (cfg.FT + cfg.KC - 1) // cfg.KC
    M2 = cfg.M2

    def lin(name, m, n=nkc):
        for kc in range(n):
            o.append((name, kc, m))
    for qq in range(cfg.NQ):
        for j in range(qq * cfg.QJ, (qq + 1) * cfg.QJ):
            if j % 2 == 0:
                lin("win", j // 2)
            o.append(("wx", 0, j))
        if qq >= 1:
            for j in range((qq - 1) * cfg.QJ, qq * cfg.QJ):
                o.append(("wk", 0, j)); o.append(("wy", 0, j))
    for j in range((cfg.NQ - 1) * cfg.QJ, cfg.NQ * cfg.QJ):
        o.append(("wk", 0, j)); o.append(("wy", 0, j))
    for m in range(M2):
        lin("glu", m)
    for b in range(cfg.NB):
        lin("win", M2 + b); o.append(("wr", 0, b)); o.append(("wi", 0, b)); lin("win", 2 * M2 + b)
    for m in range(M2):
        lin("win", 3 * M2 + m); lin("win", 4 * M2 + m); lin("pa", m); lin("pb", m)
    for m in range(M2):
        lin("wout", m)
    for f in range(cfg.F2):
        lin("gate", f); lin("up", f)
    for m in range(M2):
        lin("down", m, fkc)
    return o


VEC_NAMES = ["g_pre_mix", "s5_d", "b_glu", "conv_w0", "conv_w1", "conv_w2", "conv_w3", "conv_b",
             "lru_b_r", "lru_b_i", "lru_lam", "g_post_mix", "g_pre_ffn", "g_post_ffn"]
VI = {n: i for i, n in enumerate(VEC_NAMES)}
NV = len(VEC_NAMES)


class _Stop(Exception):
    pass


def build(cfg):
    import os
    nc = bass.Bass("TRN2", target_bir_lowering=False)
    S = Sched()
    kstop = int(os.environ.get("KSTOP", "0"))

    def ck(n):
        if kstop == n:
            raise _Stop()
    try:
        _build_body(nc, S, cfg, ck)
    except _Stop:
        pass
    return nc, S


def _build_body(nc, S, cfg, ck):
    D, KT, FT, M2, F2, G, GQ, NB = cfg.D, cfg.KT, cfg.FT, cfg.M2, cfg.F2, cfg.G, cfg.GQ, cfg.NB
    DEPTH, NS, NMAX, NP, KC, QJ, NQ = cfg.DEPTH, cfg.NS, cfg.NMAX, cfg.NP, cfg.KC, cfg.QJ, cfg.NQ
    DFF, SEQ, NSAMP, DEC = cfg.DFF, cfg.SEQ, cfg.NSAMP, cfg.DEC_SEQ
    NCHMAX = cfg.NCHMAX
    WQ = QJ * 4

    def din(name, shape):
        return nc.dram_tensor(name, list(shape), F32, kind="ExternalInput").ap()

    def dout(name, shape):
        return nc.dram_tensor(name, list(shape), F32, kind="ExternalOutput").ap()

    x_p = din("x_p", [SEQ, D]); x_s = din("x_s", [NSAMP, D])
    i_s5re = din("i_s5re", [DEPTH, NS, GQ, 128]); i_s5im = din("i_s5im", [DEPTH, NS, GQ, 128])
    i_lru = din("i_lru", [DEPTH, NS, KT, 128]); i_conv = din("i_conv", [DEPTH, NS, 3 * KT, 128])
    vec_in = {}
    for n in ["g_pre_mix", "s5_d", "b_glu", "conv_b", "lru_b_r", "lru_b_i", "lru_lam", "g_post_mix",
              "g_pre_ffn", "g_post_ffn"]:
        vec_in[n] = din(n, [DEPTH, KT, 128])
    conv_w_in = din("conv_w", [DEPTH, 4, KT, 128])
    w_in = din("w_in", [DEPTH, D, 5 * D])
    lam_re = din("s5_lam_re", [DEPTH, GQ, 128]); lam_im = din("s5_lam_im", [DEPTH, GQ, 128])
    log_step = din("s5_log_step", [DEPTH, GQ, 2])
    b_re = din("s5_b_re", [DEPTH, G, P_STATE, HG]); b_im = din("s5_b_im", [DEPTH, G, P_STATE, HG])
    c_re = din("s5_c_re", [DEPTH, G, HG, P_STATE]); c_im = din("s5_c_im", [DEPTH, G, HG, P_STATE])
    w_glu = din("w_glu", [DEPTH, D, D]); p_a = din("p_a", [DEPTH, D, D]); p_b = din("p_b", [DEPTH, D, D])
    w_out = din("w_out", [DEPTH, D, D])
    lru_w_r = din("lru_w_r", [DEPTH, NB, 256, 256]); lru_w_i = din("lru_w_i", [DEPTH, NB, 256, 256])
    w_gate = din("w_gate", [DEPTH, D, DFF]); w_up = din("w_up", [DEPTH, D, DFF]); w_down = din("w_down", [DEPTH, DFF, D])

    y_p = dout("y_p", [SEQ, D]); y_s = dout("y_s", [NSAMP, D])
    o_ps5re = dout("o_ps5re", [DEPTH, GQ, 128]); o_ps5im = dout("o_ps5im", [DEPTH, GQ, 128])
    o_plru = dout("o_plru", [DEPTH, KT, 128]); o_pconv = dout("o_pconv", [DEPTH, 3 * KT, 128])
    o_ss5re = dout("o_ss5re", [DEPTH, NS, GQ, 128]); o_ss5im = dout("o_ss5im", [DEPTH, NS, GQ, 128])
    o_slru = dout("o_slru", [DEPTH, NS, KT, 128]); o_sconv = dout("o_sconv", [DEPTH, NS, 3 * KT, 128])

    regs, wsizes = wregions(cfg)
    wscr = [{n: nc.dram_tensor("wscr%d_%s" % (l, n), [sz], BF16, kind="Internal").ap() for n, sz in wsizes.items()}
            for l in range(DEPTH)]

    es = ExitStack()

    def finalize():
        S.finish()
        keys = S.sem_keys()
        sems = {k: es.enter_context(nc.semaphore("s%d" % i)) for i, k in enumerate(keys)}
        block = es.enter_context(nc.Block())

        def make_body(ops):
            def body(eng):
                for o in ops:
                    if o[0] == "wait":
                        eng.wait_ge(sems[o[1]], o[2])
                    elif o[0] == "op":
                        ins = getattr(eng, o[1])(**o[2])
                        if o[3] is not None:
                            ins.then_inc(sems[o[3]], 1)
                    else:
                        eng.dma_start(out=o[1], in_=o[2], **o[3]).then_inc(sems[("d", o[4])], 16)
            return body

        block.sync(make_body(S.ops["sp"]))
        block.tensor(make_body(S.ops["pe"]))
        block.scalar(make_body(S.ops["act"]))
        block.vector(make_body(S.ops["dve"]))
        block.gpsimd(make_body(S.ops["pool"]))
        es.close()

    def ck2(n):
        try:
            ck(n)
        except _Stop:
            finalize()
            raise


    def sbt(name, shape, dt=F32):
        return es.enter_context(nc.sbuf_tensor(name, list(shape), dt))

    UNIT = KT * NMAX
    XOFF = 0
    HOFF = XOFF + 2 * UNIT
    AOFF = HOFF + UNIT
    need_mid = max(12 * QJ * NMAX, UNIT + 32 * NMAX, 3 * UNIT)
    need = max(2 * UNIT + need_mid, UNIT + FT * NMAX, (NP // 128 + 1) * 2 * D)
    NUNIT = (need + UNIT - 1) // UNIT
    BIGN = max(AOFF + NUNIT * UNIT, 24576)
    big = sbt("big", [128, BIGN], BF16)

    def V(off, n, dt=BF16):
        if dt == BF16:
            return big[:, off:off + n]
        return big[:, off:off + 2 * n].bitcast(F32)

    def U(i):
        return AOFF + i * UNIT

    SLOTN = 16 * 256
    wslots = [sbt("wslot%d" % i, [128, SLOTN], BF16) for i in range(cfg.NSLOT)]
    wslot_buf = [Buf("ws%d" % i) for i in range(cfg.NSLOT)]
    NTF, NTB = 10, 6
    tmpf = [sbt("tmpf%d" % i, [128, NMAX], F32) for i in range(NTF)]
    tmpf_b = [Buf() for _ in range(NTF)]
    tmpb = [sbt("tmpb%d" % i, [128, NMAX], BF16) for i in range(NTB)]
    tmpb_b = [Buf() for _ in range(NTB)]
    rr = {"f": 0, "b": 0, "ps": 0, "ev": 0}

    def tf():
        i = rr["f"]; rr["f"] = (i + 1) % NTF
        return tmpf[i], tmpf_b[i]

    def tb():
        i = rr["b"]; rr["b"] = (i + 1) % NTB
        return tmpb[i], tmpb_b[i]

    psum = [es.enter_context(nc.psum_tensor("ps%d" % i, [128, 512], F32)) for i in range(8)]
    psum_b = [Buf("ps%d" % i) for i in range(8)]
    SSB = 7

    def bank():
        i = rr["ps"]; rr["ps"] = (i + 1) % 7
        return psum[i], psum_b[i]

    identF = sbt("identF", [128, 128], F32)
    onesD = sbt("onesD", [128, 128], BF16)
    maskbd = sbt("maskbd", [128, 128], F32)
    rstd = sbt("rstd", [128, NMAX], F32); rstd_b = Buf("rstd")
    PV = [sbt("pv%d" % l, [128, NV, KT], F32) for l in range(DEPTH)]
    PVb = [Buf("pv%d" % l) for l in range(DEPTH)]
    C8 = [sbt("c8_%d" % l, [128, 2, KT], F32) for l in range(DEPTH)]
    A4 = [sbt("a4_%d" % l, [128, 2, GQ], F32) for l in range(DEPTH)]
    A4b = [Buf() for _ in range(DEPTH)]
    S5car = [sbt("s5car%d" % l, [128, 2, GQ], F32) for l in range(DEPTH)]
    S5car_b = [Buf() for _ in range(DEPTH)]
    S5ini = [[sbt("s5ini%d_%d" % (l, s), [128, 2, GQ], F32) for s in range(NS)] for l in range(DEPTH)]
    S5ini_b = [[Buf() for _ in range(NS)] for l in range(DEPTH)]
    LRUcar = [sbt("lrucar%d" % l, [128, 1 + NS, KT], F32) for l in range(DEPTH)]
    LRUcar_b = [Buf() for _ in range(DEPTH)]
    CONVcar = [sbt("convcar%d" % l, [128, 1 + NS, 3, KT], F32) for l in range(DEPTH)]
    CONVcar_b = [Buf() for _ in range(DEPTH)]
    xpad = [sbt("xpad%d" % i, [128, NMAX + 3 * (1 + NS)], F32) for i in range(2)]
    xpad_b = [Buf() for _ in range(2)]
    sct = [sbt("sct%d" % i, [128, WQ], F32) for i in range(4)]
    sct_b = [Buf() for _ in range(4)]
    iost = sbt("iost", [128, 128], F32); iost_b = Buf()

    const_b = Buf("const")
    I32 = mybir.dt.int32
    ri32 = sbt("ri32", [128, 128], I32); rpi32 = sbt("rpi32", [128, 1], I32)
    rf = sbt("rf", [128, 128], F32); rp = sbt("rp", [128, 1], F32)
    S.op("pool", "memset", writes=[const_b], ap=onesD[:], constant=1.0 / D)
    S.op("pool", "iota", writes=[const_b], out=ri32[:], pattern=[[1, 128]], base=0, channel_multiplier=0)
    S.op("pool", "iota", writes=[const_b], out=rpi32[:], pattern=[[0, 1]], base=0, channel_multiplier=1)
    S.op("dve", "tensor_copy", [const_b], [const_b], out=rf[:], in_=ri32[:])
    S.op("dve", "tensor_copy", [const_b], [const_b], out=rp[:], in_=rpi32[:])
    S.op("dve", "tensor_scalar", [const_b], [const_b], out=identF[:], in0=rf[:], scalar1=rp[:, 0:1], scalar2=None,
         op0=ALU.is_equal)
    S.op("dve", "tensor_single_scalar", [const_b], [const_b], out=ri32[:], in_=ri32[:], scalar=4,
         op=ALU.arith_shift_right)
    S.op("dve", "tensor_single_scalar", [const_b], [const_b], out=rpi32[:], in_=rpi32[:], scalar=4,
         op=ALU.arith_shift_right)
    S.op("dve", "tensor_copy", [const_b], [const_b], out=rf[:], in_=ri32[:])
    S.op("dve", "tensor_copy", [const_b], [const_b], out=rp[:], in_=rpi32[:])
    S.op("dve", "tensor_scalar", [const_b], [const_b], out=maskbd[:], in0=rf[:], scalar1=rp[:, 0:1], scalar2=None,
         op0=ALU.is_equal)

    ck2(1)
    def evcopy(out, in_, reads, writes, eng=None):
        if eng is None:
            eng = ("act", "dve")[rr["ev"] % 2]; rr["ev"] += 1
        if eng == "act":
            S.op("act", "copy", reads, writes, out=out, in_=in_)
        else:
            S.op(eng, "tensor_copy", reads, writes, out=out, in_=in_)

    def transpose_to(out_sb, in_sb, rows, cols, reads, writes, eng=None):
        pb, pbb = bank()
        S.op("pe", "transpose", reads=list(reads) + [const_b], writes=[pbb], out=pb[0:cols, 0:rows], in_=in_sb,
             identity=identF[0:rows, 0:rows])
        evcopy(out_sb, pb[0:cols, 0:rows], [pbb], writes, eng)

    vstage = sbt("vstage", [128, 128], F32); vstage_b = Buf()
    for l in range(DEPTH):
        rows_total = NV * KT
        srcs = []
        for n in VEC_NAMES:
            if n.startswith("conv_w"):
                srcs.append(conv_w_in[l, int(n[-1])])
            else:
                srcs.append(vec_in[n][l])
        r0 = 0
        while r0 < rows_total:
            nr = min(128, rows_total - r0)
            r = r0
            while r < r0 + nr:
                vi, k0 = divmod(r, KT)
                cnt = min(KT - k0, r0 + nr - r)
                S.dma("sp", vstage[r - r0:r - r0 + cnt, :], srcs[vi][k0:k0 + cnt, :], writes=[vstage_b])
                r += cnt
            outv = PV[l][:].rearrange("p v k -> p (v k)")[:, r0:r0 + nr]
            transpose_to(outv, vstage[0:nr, :], nr, 128, [vstage_b], [PVb[l]])
            r0 += nr
        lamv = PV[l][:, VI["lru_lam"], :]
        t0, t0b = tf()
        S.op("act", "activation", [PVb[l]], [t0b], out=t0[:, 0:KT], in_=lamv, func=AF.Exp, scale=-1.0)
        S.op("act", "activation", [t0b], [t0b], out=t0[:, 0:KT], in_=t0[:, 0:KT], func=AF.Ln, bias=1.0, scale=1.0)
        S.op("dve", "tensor_scalar", [t0b], [PVb[l]], out=C8[l][:, 0, :], in0=t0[:, 0:KT], scalar1=-8.0, scalar2=None,
             op0=ALU.mult)
        S.op("dve", "tensor_scalar", [t0b], [PVb[l]], out=C8[l][:, 1, :], in0=t0[:, 0:KT], scalar1=-16.0, scalar2=None,
             op0=ALU.mult)
        S.op("pool", "memset", writes=[S5car_b[l]], ap=S5car[l][:], constant=0.0)
        S.op("pool", "memset", writes=[LRUcar_b[l]], ap=LRUcar[l][:], constant=0.0)
        S.op("pool", "memset", writes=[CONVcar_b[l]], ap=CONVcar[l][:], constant=0.0)
        for s in range(NS):
            for ri, src in enumerate([i_s5re, i_s5im]):
                S.dma("sp", vstage[0:GQ, :], src[l, s], writes=[vstage_b])
                transpose_to(S5ini[l][s][:, ri, :], vstage[0:GQ, :], GQ, 128, [vstage_b], [S5ini_b[l][s]])
            S.dma("sp", vstage[0:KT, :], i_lru[l, s], writes=[vstage_b])
            transpose_to(LRUcar[l][:, 1 + s, :], vstage[0:KT, :], KT, 128, [vstage_b], [LRUcar_b[l]])
            S.dma("sp", vstage[0:3 * KT, :], i_conv[l, s], writes=[vstage_b])
            transpose_to(CONVcar[l][:, 1 + s, :, :].rearrange("p a k -> p (a k)"), vstage[0:3 * KT, :], 3 * KT, 128,
                         [vstage_b], [CONVcar_b[l]])

    ck2(2)
    CW = 2048
    NST = 3
    pst_f = [V(XOFF + i * 2 * CW, CW, F32) for i in range(NST)]
    pst_fb = [Buf() for _ in range(NST)]
    pst_h = [V(XOFF + NST * 2 * CW + i * CW, CW, BF16) for i in range(NST)]
    pst_hb = [Buf() for _ in range(NST)]
    assert NST * 3 * CW <= BIGN, "prepass staging does not fit"
    pp = {"i": 0}

    def prepass_piece(src_ap, ncols, dst_ap, view=None):
        i = pp["i"]; pp["i"] += 1
        sf, sfb = pst_f[i % NST], pst_fb[i % NST]
        sh, shb = pst_h[i % NST], pst_hb[i % NST]
        if view is None:
            S.dma("sp", sf[:, 0:ncols], src_ap, writes=[sfb])
        else:
            for bb, sv in enumerate(src_ap):
                S.dma("sp", sf[:, bb * 512:(bb + 1) * 512].rearrange("p (k c) -> p k c", k=2), sv, writes=[sfb])
        eng = ("dve", "act", "pool")[i % 3]
        evcopy(sh[:, 0:ncols], sf[:, 0:ncols], [sfb], [shb], eng)
        S.dma("sp", dst_ap, sh[:, 0:ncols].rearrange("p (m c) -> p m c", c=dst_ap.shape[-1]), reads=[shb],
              writes=[wreg_b[cur_layer[0]]], arena=True)

    wreg_b = [Buf("wreg%d" % l, region=True) for l in range(DEPTH)]
    cur_layer = [0]
    for l in range(DEPTH):
        cur_layer[0] = l
        for name, src, K, M in [("win", w_in, D, 5 * D), ("glu", w_glu, D, D), ("pa", p_a, D, D), ("pb", p_b, D, D),
                                ("wout", w_out, D, D), ("gate", w_gate, D, DFF), ("up", w_up, D, DFF),
                                ("down", w_down, DFF, D)]:
            for kt in range(K // 128):
                base, nk, cnt = regs[(name, kt // KC)]
                CH = 128 * nk * 256
                c0 = 0
                while c0 < M:
                    cw = min(CW, M - c0)
                    dst = dram_ap(wscr[l][name], base + (c0 // 256) * CH + (kt % KC) * 256,
                                  [[nk * 256, 128], [CH, cw // 256], [1, 256]])
                    prepass_piece(src[l, kt * 128:(kt + 1) * 128, c0:c0 + cw], cw, dst)
                    c0 += cw
        for name, src in [("wr", lru_w_r), ("wi", lru_w_i)]:
            base, nk, cnt = regs[(name, 0)]
            b0 = 0
            while b0 < NB:
                nb = min(4, NB - b0)
                srcv = [src[l, b0 + bb].rearrange("(k p) c -> p k c", p=128) for bb in range(nb)]
                dst = dram_ap(wscr[l][name], base + b0 * 128 * 512, [[512, 128], [128 * 512, nb], [1, 512]])
                prepass_piece(srcv, nb * 512, dst, view="blocks")
                b0 += nb
    S.barrier()

    ck2(3)
    NGH = GQ * HG

    def s5_prep(l):
        off = [0]

        def alloc(n, dt=F32):
            o = off[0]; off[0] += n * (2 if dt == F32 else 1)
            assert off[0] <= BIGN, "s5 prep scratch overflow"
            return V(o, n, dt), Buf()

        def sm():
            return alloc(GQ)

        def v3(a):
            return a.rearrange("p (g h) -> p g h", h=HG)

        def bc(a):
            return a.unsqueeze(2).broadcast_to([128, GQ, HG])

        u1f, u1b = alloc(NGH); u2f, u2b = alloc(NGH)

        def cmul(o, a, b, negI=False, three=False):
            oR, oRb, oI, oIb = o; aR, aRb, aI, aIb = a; bR, bRb, bI, bIb = b
            if three:
                u1 = v3(u1f); u2 = v3(u2f)
            else:
                u1 = u1f[:, 0:GQ]; u2 = u2f[:, 0:GQ]
            S.op("dve", "tensor_tensor", [aRb, bRb], [u1b], out=u1, in0=aR, in1=bR, op=ALU.mult)
            S.op("dve", "tensor_tensor", [aIb, bIb], [u2b], out=u2, in0=aI, in1=bI, op=ALU.mult)
            S.op("dve", "tensor_tensor", [u1b, u2b], [oRb], out=oR, in0=u1, in1=u2, op=ALU.subtract)
            S.op("dve", "tensor_tensor", [aRb, bIb], [u1b], out=u1, in0=aR, in1=bI, op=ALU.mult)
            S.op("dve", "tensor_tensor", [aIb, bRb], [u2b], out=u2, in0=aI, in1=bR, op=ALU.mult)
            if negI:
                S.op("dve", "scalar_tensor_tensor", [u1b, u2b], [oIb], out=oI, in0=u1, scalar=-1.0, in1=u2,
                     op0=ALU.mult, op1=ALU.subtract)
            else:
                S.op("dve", "tensor_tensor", [u1b, u2b], [oIb], out=oI, in0=u1, in1=u2, op=ALU.add)

        tin, tinb = alloc(3 * 128)
        tin3 = tin.rearrange("p (a c) -> p a c", a=3)
        ls2, ls2b = alloc(2)
        S.dma("sp", tin3[0:GQ, 0, :], lam_re[l], writes=[tinb], arena=True)
        S.dma("sp", tin3[0:GQ, 1, :], lam_im[l], writes=[tinb], arena=True)
        S.dma("sp", ls2[0:GQ, :], log_step[l], writes=[ls2b], arena=True)
        S.op("dve", "tensor_copy", [ls2b, tinb], [tinb], out=tin3[0:GQ, 2, :].rearrange("p (a c) -> p a c", a=2),
             in_=ls2[0:GQ, :].unsqueeze(2).broadcast_to([GQ, 2, 64]))
        lamR, lamRb = sm(); lamI, lamIb = sm(); lst, lstb = sm()
        for a_, (dst, dstb) in enumerate([(lamR, lamRb), (lamI, lamIb), (lst, lstb)]):
            transpose_to(dst, tin3[0:GQ, a_, :], GQ, 128, [tinb], [dstb])
        st, stb = sm(); lrs, lrsb = sm(); ang, angb = sm()
        S.op("act", "activation", [lstb], [stb], out=st, in_=lst, func=AF.Exp)
        S.op("dve", "tensor_tensor", [stb, lamRb], [lrsb], out=lrs, in0=lamR, in1=st, op=ALU.mult)
        S.op("dve", "tensor_tensor", [stb, lamIb], [angb], out=ang, in0=lamI, in1=st, op=ALU.mult)
        mg, mgb = sm(); sn, snb = sm(); cs, csb = sm()
        S.op("act", "activation", [lrsb], [mgb], out=mg, in_=lrs, func=AF.Exp, scale=1.0 / 32)
        S.op("act", "activation", [angb], [snb], out=sn, in_=ang, func=AF.Sin, scale=1.0 / 32)
        halfpi, hpb = alloc(1)
        S.op("pool", "memset", writes=[hpb], ap=halfpi, constant=math.pi / 2)
        S.op("act", "activation", [angb, hpb], [csb], out=cs, in_=ang, func=AF.Sin, scale=1.0 / 32, bias=halfpi[:, 0:1])
        Ak = [None] * 5
        for k in range(1, 5):
            r_, rb_ = sm(); i_, ib_ = sm()
            Ak[k] = (r_, rb_, i_, ib_)
        t1, t1b = sm(); t2, t2b = sm()
        S.op("dve", "tensor_tensor", [mgb, csb], [Ak[1][1]], out=Ak[1][0], in0=mg, in1=cs, op=ALU.mult)
        S.op("dve", "tensor_tensor", [mgb, snb], [Ak[1][3]], out=Ak[1][2], in0=mg, in1=sn, op=ALU.mult)
        tt_ = (t1, t1b, t2, t2b)
        for it in range(5):
            cmul(tt_, Ak[1], Ak[1])
            S.op("dve", "tensor_copy", [t1b], [Ak[1][1]], out=Ak[1][0], in_=t1)
            S.op("dve", "tensor_copy", [t2b], [Ak[1][3]], out=Ak[1][2], in_=t2)
        cmul(Ak[2], Ak[1], Ak[1]); cmul(Ak[3], Ak[2], Ak[1]); cmul(Ak[4], Ak[2], Ak[2])
        S.op("pool", "tensor_copy", [Ak[4][1]], [A4b[l]], out=A4[l][:, 0, :], in_=Ak[4][0])
        S.op("pool", "tensor_copy", [Ak[4][3]], [A4b[l]], out=A4[l][:, 1, :], in_=Ak[4][2])
        nR, nRb = sm(); den, denb = sm(); cfR, cfRb = sm(); cfI, cfIb = sm()
        A1R, A1Rb, A1I, A1Ib = Ak[1]
        S.op("dve", "tensor_scalar", [A1Rb], [nRb], out=nR, in0=A1R, scalar1=-1.0, scalar2=None, op0=ALU.add)
        S.op("dve", "tensor_tensor", [lamRb], [t1b], out=t1, in0=lamR, in1=lamR, op=ALU.mult)
        S.op("dve", "tensor_tensor", [lamIb], [t2b], out=t2, in0=lamI, in1=lamI, op=ALU.mult)
        S.op("dve", "tensor_tensor", [t1b, t2b], [denb], out=den, in0=t1, in1=t2, op=ALU.add)
        S.op("dve", "reciprocal", [denb], [denb], out=den, in_=den)
        S.op("dve", "tensor_tensor", [nRb, lamRb], [t1b], out=t1, in0=nR, in1=lamR, op=ALU.mult)
        S.op("dve", "tensor_tensor", [A1Ib, lamIb], [t2b], out=t2, in0=A1I, in1=lamI, op=ALU.mult)
        S.op("dve", "tensor_tensor", [t1b, t2b], [t1b], out=t1, in0=t1, in1=t2, op=ALU.add)
        S.op("dve", "tensor_tensor", [t1b, denb], [cfRb], out=cfR, in0=t1, in1=den, op=ALU.mult)
        S.op("dve", "tensor_tensor", [A1Ib, lamRb], [t1b], out=t1, in0=A1I, in1=lamR, op=ALU.mult)
        S.op("dve", "tensor_tensor", [nRb, lamIb], [t2b], out=t2, in0=nR, in1=lamI, op=ALU.mult)
        S.op("dve", "tensor_tensor", [t1b, t2b], [t1b], out=t1, in0=t1, in1=t2, op=ALU.subtract)
        S.op("dve", "tensor_tensor", [t1b, denb], [cfIb], out=cfI, in0=t1, in1=den, op=ALU.mult)

        def bcA(k):
            r_, rb_, i_, ib_ = Ak[k]
            return (bc(r_), rb_, bc(i_), ib_)

        LZ = []
        for ri in range(2):
            lz, lzb = alloc(KT * 128, BF16)
            S.op("pool", "memset", writes=[lzb], ap=lz, constant=0.0)
            LZ.append((lz, lzb))
        mark = off[0]
        zt, ztb = alloc(4096, BF16)
        wxz_b = Buf()
        S.op("pool", "memset", writes=[ztb], ap=zt, constant=0.0)
        basex, _, _ = regs[("wx", 0)]
        for j in range(KT):
            dst = dram_ap(wscr[l]["wx"], basex + j * 128 * 4096, [[4096, 128], [1, 4096]])
            S.dma("sp", dst, zt, reads=[ztb], writes=[wreg_b[l], wxz_b], arena=True)
        BR, BRb = alloc(NGH); BI, BIb = alloc(NGH)
        for src, dst, dstb in [(b_re, BR, BRb), (b_im, BI, BIb)]:
            for g2 in range(2):
                srcv = dram_ap(src, l * G * P_STATE * HG + g2 * P_STATE * HG,
                               [[HG, P_STATE], [2 * P_STATE * HG, GQ], [1, HG]])
                S.dma("sp", v3(dst)[g2 * 64:(g2 + 1) * 64, :, :], srcv, writes=[dstb], arena=True)
        BbR, BbRb = alloc(NGH); BbI, BbIb = alloc(NGH)
        Bb3 = (v3(BbR), BbRb, v3(BbI), BbIb)
        cmul(Bb3, (bc(cfR), cfRb, bc(cfI), cfIb), (v3(BR), BRb, v3(BI), BIb), three=True)
        for ri, (src, srcb) in enumerate([(BbR, BbRb), (BbI, BbIb)]):
            lz5 = LZ[ri][0].rearrange("p (j q a h) -> p j q a h", j=KT, q=4, a=2)
            s4 = src.rearrange("p (j q h) -> p j q h", j=KT, q=4)
            for g2 in range(2):
                S.op("dve", "tensor_copy", [srcb, LZ[ri][1]], [LZ[ri][1]], out=lz5[g2 * 64:(g2 + 1) * 64, :, :, g2, :],
                     in_=s4[g2 * 64:(g2 + 1) * 64, :, :, :])
        wxs, wxsb = alloc(NGH); wxi, wxib = alloc(NGH)
        stg, stgb = alloc(KT * 2 * 128, BF16)
        stg4 = stg.rearrange("p (j r c) -> p j r c", j=KT, r=2)
        for s in range(4):
            k = 3 - s
            if k == 0:
                cur = [(BbR, BbRb), (BbI, BbIb)]
            else:
                cmul((v3(wxs), wxsb, v3(wxi), wxib), bcA(k), Bb3, three=True)
                cur = [(wxs, wxsb), (wxi, wxib)]
            for ri in range(2):
                srcw, srcwb = cur[ri]
                s3 = srcw.rearrange("p (j c) -> p j c", j=KT)
                for j in range(KT):
                    pb, pbb = bank()
                    S.op("pe", "transpose", reads=[srcwb, const_b], writes=[pbb], out=pb[0:64, 0:128], in_=s3[:, j, :],
                         identity=identF[:, :])
                    evcopy(stg4[0:64, j, ri, :], pb[0:64, 0:128], [pbb], [stgb])
            for q in range(4):
                for g2 in range(2):
                    for ri in range(2):
                        dst = dram_ap(wscr[l]["wx"], basex + (q * 32 + g2 * 16) * 4096 + s * 1024 + q * 256 + ri * 128 + g2 * 64,
                                      [[4096, 16], [128 * 4096, KT], [1, 64]])
                        S.dma("sp", dst, stg4[q * 16:(q + 1) * 16, :, ri, g2 * 64:(g2 + 1) * 64], reads=[stgb, wxz_b],
                              writes=[wreg_b[l]], arena=True)
        S.barrier()
        off[0] = mark
        CR, CRb = alloc(NGH); CI, CIb = alloc(NGH)
        cin, cinb = alloc(KT * 128)
        cin4 = cin.rearrange("p (j a c) -> p j a c", j=KT, a=2)
        for src, dst, dstb in [(c_re, CR, CRb), (c_im, CI, CIb)]:
            for q in range(4):
                for g2 in range(2):
                    srcv = dram_ap(src, l * G * HG * P_STATE + (2 * q + g2) * HG * P_STATE,
                                   [[P_STATE, HG], [8 * HG * P_STATE, KT], [1, P_STATE]])
                    S.dma("sp", cin4[q * 16:(q + 1) * 16, :, g2, :], srcv, writes=[cinb], arena=True)
            d3 = dst.rearrange("p (j c) -> p j c", j=KT)
            for j in range(KT):
                pb, pbb = bank()
                S.op("pe", "transpose", reads=[cinb, const_b], writes=[pbb], out=pb[0:128, 0:64],
                     in_=cin4[0:64, j, :, :].rearrange("p a c -> p (a c)"), identity=identF[0:64, 0:64])
                evcopy(d3[:, j, :], pb[0:128, 0:64], [pbb], [dstb])
        caR, caRb = alloc(NGH); caI, caIb = alloc(NGH)
        RZ = []
        for ri in range(2):
            rz, rzb = alloc(KT * 128, BF16)
            S.op("pool", "memset", writes=[rzb], ap=rz, constant=0.0)
            RZ.append((rz, rzb))
        wkst, wkstb = alloc(KT * 128, BF16)
        wkst3 = wkst.rearrange("p (j c) -> p j c", j=KT)
        JG = min(8, KT)
        wyst, wystb = alloc(JG * 1024, BF16)
        S.op("pool", "memset", writes=[wystb], ap=wyst, constant=0.0)
        wyst5 = wyst.rearrange("p (j q r c) -> p j q r c", j=JG, q=4, r=2)
        basek, _, _ = regs[("wk", 0)]
        basey, _, _ = regs[("wy", 0)]
        C3 = (v3(CR), CRb, v3(CI), CIb)
        for k in range(5):
            if k == 0:
                S.op("dve", "tensor_copy", [CRb], [caRb], out=caR, in_=CR)
                S.op("dve", "tensor_scalar", [CIb], [caIb], out=caI, in0=CI, scalar1=-1.0, scalar2=None, op0=ALU.mult)
            else:
                cmul((v3(caR), caRb, v3(caI), caIb), bcA(k), C3, negI=True, three=True)
            for ri, (src, srcb) in enumerate([(caR, caRb), (caI, caIb)]):
                rz5 = RZ[ri][0].rearrange("p (j q a h) -> p j q a h", j=KT, q=4, a=2)
                s4 = src.rearrange("p (j q h) -> p j q h", j=KT, q=4)
                for g2 in range(2):
                    S.op("dve", "tensor_copy", [srcb, RZ[ri][1]], [RZ[ri][1]], out=rz5[g2 * 64:(g2 + 1) * 64, :, :, g2, :],
                         in_=s4[g2 * 64:(g2 + 1) * 64, :, :, :])
            rz3 = [RZ[ri][0].rearrange("p (j c) -> p j c", j=KT) for ri in range(2)]
            lz3 = [LZ[ri][0].rearrange("p (j c) -> p j c", j=KT) for ri in range(2)]
            if k <= 3:
                for j in range(KT):
                    pb, pbb = bank()
                    S.op("pe", "matmul", reads=[LZ[0][1], RZ[0][1]], writes=[pbb], inc=False, out=pb[:, 0:128],
                         lhsT=lz3[0][:, j, :], rhs=rz3[0][:, j, :], start=True, stop=False)
                    S.op("pe", "matmul", reads=[LZ[1][1], RZ[1][1]], writes=[pbb], out=pb[:, 0:128],
                         lhsT=lz3[1][:, j, :], rhs=rz3[1][:, j, :], start=False, stop=True)
                    S.op("dve", "tensor_tensor", [pbb, const_b], [wkstb], out=wkst3[:, j, :], in0=pb[:, 0:128],
                         in1=maskbd[:], op=ALU.mult)
                dst = dram_ap(wscr[l]["wk"], basek + k * 128, [[512, 128], [128 * 512, KT], [1, 128]])
                S.dma("sp", dst, wkst3, reads=[wkstb], writes=[wreg_b[l]], arena=True)
            if k >= 1:
                t = k - 1
                for jg0 in range(0, KT, JG):
                    for ri in range(2):
                        for q in range(4):
                            S.op("pool", "tensor_copy", [RZ[ri][1], wystb], [wystb],
                                 out=wyst5[:, :, q, ri, q * 32:(q + 1) * 32],
                                 in_=rz3[ri][:, jg0:jg0 + JG, q * 32:(q + 1) * 32])
                    dst = dram_ap(wscr[l]["wy"], basey + jg0 * 128 * 4096 + t * 1024, [[4096, 128], [128 * 4096, JG], [1, 1024]])
                    S.dma("sp", dst, wyst.rearrange("p (j f) -> p j f", j=JG), reads=[wystb], writes=[wreg_b[l]],
                          arena=True)
        S.barrier()

    for l in range(DEPTH):
        s5_prep(l)

    ck2(4)
    ARENA = NUNIT * UNIT
    A_UDE = AOFF
    A_YA = AOFF + UNIT
    A_S5 = AOFF + 2 * UNIT
    SZX = 2 * WQ * NCHMAX
    SZH = WQ * NCHMAX
    assert 4 * SZX + 4 * SZH <= ARENA - 2 * UNIT, "S5 buffers do not fit"
    NET = 16
    assert UNIT + NET * 2 * NMAX <= ARENA - 2 * UNIT and 3 * UNIT <= ARENA - 2 * UNIT
    assert FT * NMAX <= ARENA - UNIT

    xF = V(XOFF, KT * NMAX, F32).rearrange("p (k n) -> p k n", k=KT)
    x_b = [Buf("x%d" % k) for k in range(KT)]
    hB = V(HOFF, KT * NMAX).rearrange("p (k n) -> p k n", k=KT)
    h_b = [Buf("h%d" % k) for k in range(KT)]

    def unit3(off):
        return V(off, KT * NMAX).rearrange("p (k n) -> p k n", k=KT)

    seq = []
    worder = weight_order(cfg)
    for ti in range(cfg.NT):
        for l in range(DEPTH):
            for (name, kc, m) in worder:
                seq.append((l, name, kc, m))
    wst = {"use": 0, "iss": 0, "done": 0}

    def w_pump():
        while wst["iss"] < len(seq) and wst["iss"] < wst["done"] + cfg.NSLOT:
            i = wst["iss"]
            l, name, kc, m = seq[i]
            base, nk, cnt = regs[(name, kc)]
            sl = i % cfg.NSLOT
            src = dram_ap(wscr[l][name], base + m * 128 * nk * 256, [[nk * 256, 128], [1, nk * 256]])
            S.dma("sp", wslots[sl][:, 0:nk * 256], src, reads=[wreg_b[l]], writes=[wslot_buf[sl]])
            wst["iss"] += 1

    def w_get(l, name, kc, m):
        i = wst["use"]
        assert seq[i] == (l, name, kc, m), (i, seq[i], (l, name, kc, m))
        assert i < wst["done"] + cfg.NSLOT, "too many live weight chunks"
        w_pump()
        wst["use"] += 1
        nk = regs[(name, kc)][1]
        return wslots[i % cfg.NSLOT], wslot_buf[i % cfg.NSLOT], nk

    def w_done(n=1):
        wst["done"] += n
        w_pump()

    def linear(l, name, m, rhs_fn, nkt, N):
        nkc = (nkt + KC - 1) // KC
        bks = [bank(), bank()]
        for kc in range(nkc):
            sl, slb, nk = w_get(l, name, kc, m)
            w3 = sl[:, 0:nk * 256].rearrange("p (k c) -> p k c", c=256)
            for mi in range(2):
                for k in range(nk):
                    kt = kc * KC + k
                    r, rb = rhs_fn(kt)
                    S.op("pe", "matmul", reads=[slb, rb], writes=[bks[mi][1]], inc=(k == nk - 1), out=bks[mi][0][:, 0:N],
                         lhsT=w3[:, k, mi * 128:(mi + 1) * 128], rhs=r, start=(kt == 0), stop=(kt == nkt - 1))
            w_done()
        return bks

    def calc_rstd(N):
        ssb, ssbb = psum[SSB], psum_b[SSB]
        S.op("act", "activation", [ssbb], [rstd_b], out=rstd[:, 0:N], in_=ssb[:, 0:N], func=AF.Sqrt, bias=EPS, scale=1.0)
        S.op("dve", "reciprocal", [rstd_b], [rstd_b], out=rstd[:, 0:N], in_=rstd[:, 0:N])

    def sumsq_acc(src_ap, src_b, first, last, N):
        sq, sqb = tb()
        S.op("act", "activation", [src_b], [sqb], out=sq[:, 0:N], in_=src_ap, func=AF.Square)
        S.op("pe", "matmul", [sqb, const_b], [psum_b[SSB]], inc=True, out=psum[SSB][:, 0:N], lhsT=onesD[:], rhs=sq[:, 0:N],
             start=first, stop=last)

    def prenorm(l, vname, N):
        for kt in range(KT):
            sumsq_acc(xF[:, kt, 0:N], x_b[kt], kt == 0, kt == KT - 1, N)
        calc_rstd(N)
        for kt in range(KT):
            eng = "dve"
            S.op(eng, "scalar_tensor_tensor", [x_b[kt], rstd_b, PVb[l]], [h_b[kt]], out=hB[:, kt, 0:N], in0=xF[:, kt, 0:N],
                 scalar=PV[l][:, VI[vname], kt:kt + 1], in1=rstd[:, 0:N], op0=ALU.mult, op1=ALU.mult)

    def residual(l, vname, src3, src_b, N):
        calc_rstd(N)
        for kt in range(KT):
            t_, t_b = tf()
            S.op("dve", "scalar_tensor_tensor", [src_b[kt], rstd_b, PVb[l]], [t_b], out=t_[:, 0:N], in0=src3[:, kt, 0:N],
                 scalar=PV[l][:, VI[vname], kt:kt + 1], in1=rstd[:, 0:N], op0=ALU.mult, op1=ALU.mult)
            S.op("pool", "tensor_tensor", [t_b, x_b[kt]], [x_b[kt]], out=xF[:, kt, 0:N], in0=xF[:, kt, 0:N], in1=t_[:, 0:N],
                 op=ALU.add)

    def hrhs(N):
        return lambda kt: (hB[:, kt, 0:N], h_b[kt])

    def mixer(l, N, segs):
        NCH = N // T
        ude_b = [Buf() for _ in range(KT)]
        yaB = unit3(A_YA); ya_b = [Buf() for _ in range(KT)]

        def ude(j):
            return V(A_UDE + j * NMAX, N).rearrange("p (s c) -> p s c", s=T)

        def udeflat(j):
            return V(A_UDE + j * NMAX, N)

        XR = [V(A_S5 + ss * 2 * SZX, WQ * NCHMAX, F32).rearrange("p (w c) -> p w c", w=WQ) for ss in range(2)]
        XI = [V(A_S5 + ss * 2 * SZX + SZX, WQ * NCHMAX, F32).rearrange("p (w c) -> p w c", w=WQ) for ss in range(2)]
        HB0 = A_S5 + 4 * SZX
        HbR = [V(HB0 + ss * 2 * SZH, WQ * NCHMAX).rearrange("p (w c) -> p w c", w=WQ) for ss in range(2)]
        HbI = [V(HB0 + ss * 2 * SZH + SZH, WQ * NCHMAX).rearrange("p (w c) -> p w c", w=WQ) for ss in range(2)]
        X_b = [[Buf(), Buf()] for _ in range(2)]
        Hb_b = [[Buf(), Buf()] for _ in range(2)]

        prenorm(l, "g_pre_mix", N)
        ck2(100)
        bks_hold = {}

        def phaseA(qq):
            ss = qq % 2
            for j in range(qq * QJ, (qq + 1) * QJ):
                if j % 2 == 0:
                    bks_hold["b"] = linear(l, "win", j // 2, hrhs(N), KT, N)
                pb, pbb = bks_hold["b"][j % 2]
                S.op("act", "copy", [pbb], [ude_b[j]], out=ude(j), in_=pb[:, 0:N].rearrange("p (c s) -> p s c", s=T))
                sl, slb, nk = w_get(l, "wx", 0, j)
                wx5 = sl[:, 0:4096].rearrange("p (s q r c) -> p s q r c", s=4, q=4, r=2)
                jl = j - qq * QJ
                for ri in range(2):
                    xb_, xbb = bank()
                    for q in range(4):
                        for s in range(4):
                            S.op("pe", "matmul", [slb, ude_b[j]], [xbb], inc=(q == 3 and s == 3),
                                 out=xb_[:, q * NCH:(q + 1) * NCH], lhsT=wx5[:, s, q, ri, :], rhs=ude(j)[:, s, :],
                                 start=(s == 0), stop=(s == 3))
                    dst = (XR if ri == 0 else XI)[ss][:, jl * 4:(jl + 1) * 4, 0:NCH]
                    evcopy(dst, xb_[:, 0:4 * NCH].rearrange("p (q c) -> p q c", q=4), [xbb], [X_b[ss][ri]])
                w_done()

        def phaseB(qq):
            ss = qq % 2
            w0 = qq * WQ
            AR = A4[l][:, 0, w0:w0 + WQ]; AI = A4[l][:, 1, w0:w0 + WQ]
            XRs, XIs = XR[ss], XI[ss]
            xbR, xbI = X_b[ss]
            for seg in segs:
                cs = seg["c0"] // T; nch = seg["L"] // T
                if seg["kind"] == "p":
                    it, ib = S5car[l], S5car_b[l]
                else:
                    it, ib = S5ini[l][seg["idx"]], S5ini_b[l][seg["idx"]]
                iR = it[:, 0, w0:w0 + WQ]; iI = it[:, 1, w0:w0 + WQ]
                S.op("pool", "tensor_copy", [ib], [Hb_b[ss][0]], out=HbR[ss][:, :, cs], in_=iR)
                S.op("pool", "tensor_copy", [ib], [Hb_b[ss][1]], out=HbI[ss][:, :, cs], in_=iI)
                for ci in range(nch):
                    c = cs + ci
                    if ci == 0:
                        pR, pI, pr = iR, iI, [ib]
                    else:
                        pR, pI, pr = XRs[:, :, c - 1], XIs[:, :, c - 1], [xbR, xbI]
                    t1, t2, t3, t4 = [sct[i][:, :] for i in range(4)]
                    b1, b2, b3, b4 = sct_b
                    S.op("pool", "tensor_tensor", pr + [A4b[l]], [b1], out=t1, in0=AR, in1=pR, op=ALU.mult)
                    S.op("pool", "tensor_tensor", pr + [A4b[l]], [b2], out=t2, in0=AI, in1=pI, op=ALU.mult)
                    S.op("pool", "tensor_tensor", pr + [A4b[l]], [b3], out=t3, in0=AR, in1=pI, op=ALU.mult)
                    S.op("pool", "tensor_tensor", pr + [A4b[l]], [b4], out=t4, in0=AI, in1=pR, op=ALU.mult)
                    S.op("pool", "tensor_tensor", [b1, b2], [b1], out=t1, in0=t1, in1=t2, op=ALU.subtract)
                    S.op("pool", "tensor_tensor", [b3, b4], [b3], out=t3, in0=t3, in1=t4, op=ALU.add)
                    S.op("pool", "tensor_tensor", [b1, xbR], [xbR], out=XRs[:, :, c], in0=XRs[:, :, c], in1=t1, op=ALU.add)
                    S.op("pool", "tensor_tensor", [b3, xbI], [xbI], out=XIs[:, :, c], in0=XIs[:, :, c], in1=t3, op=ALU.add)
                if nch > 1:
                    S.op("act", "copy", [xbR], [Hb_b[ss][0]], out=HbR[ss][:, :, cs + 1:cs + nch], in_=XRs[:, :, cs:cs + nch - 1])
                    S.op("act", "copy", [xbI], [Hb_b[ss][1]], out=HbI[ss][:, :, cs + 1:cs + nch], in_=XIs[:, :, cs:cs + nch - 1])
                S.op("pool", "tensor_copy", [xbR, Hb_b[ss][0]], [ib], out=iR, in_=XRs[:, :, cs + nch - 1])
                S.op("pool", "tensor_copy", [xbI, Hb_b[ss][1]], [ib], out=iI, in_=XIs[:, :, cs + nch - 1])

        def phaseC(qq):
            ss = qq % 2
            for j in range(qq * QJ, (qq + 1) * QJ):
                slk, slkb, _ = w_get(l, "wk", 0, j)
                sly, slyb, _ = w_get(l, "wy", 0, j)
                wk3 = slk[:, 0:512].rearrange("p (t c) -> p t c", t=4)
                wy5 = sly[:, 0:4096].rearrange("p (t q r c) -> p t q r c", t=4, q=4, r=2)
                yb_, ybb = bank()
                jl = j - qq * QJ
                for t in range(T):
                    mm = []
                    for s in range(t + 1):
                        mm.append((wk3[:, t - s, :], slkb, ude(j)[:, s, :], ude_b[j]))
                    for q in range(4):
                        for ri in range(2):
                            hb = (HbR if ri == 0 else HbI)[ss]
                            mm.append((wy5[:, t, q, ri, :], slyb, hb[:, jl * 4 + q, 0:NCH], Hb_b[ss][ri]))
                    for i, (lt, ltb, r, rb_) in enumerate(mm):
                        last = i == len(mm) - 1
                        S.op("pe", "matmul", [ltb, rb_], [ybb], inc=(last and t == T - 1), out=yb_[:, t * NCH:(t + 1) * NCH],
                             lhsT=lt, rhs=r, start=(i == 0), stop=last)
                w_done(2)
                tv, tvb = tf()
                S.op("dve", "scalar_tensor_tensor", [ude_b[j], ybb, PVb[l]], [tvb], out=tv[:, 0:N], in0=udeflat(j),
                     scalar=PV[l][:, VI["s5_d"], j:j + 1], in1=yb_[:, 0:N], op0=ALU.mult, op1=ALU.add)
                S.op("act", "activation", [tvb], [ya_b[j]], out=yaB[:, j, 0:N].rearrange("p (c s) -> p s c", s=T),
                     in_=tv[:, 0:N].rearrange("p (s c) -> p s c", s=T), func=AF.Gelu_apprx_tanh)

        phaseA(0)
        ck2(101)
        for qq in range(1, NQ):
            phaseA(qq); phaseB(qq - 1); phaseC(qq - 1)
        ck2(102)
        phaseB(NQ - 1); phaseC(NQ - 1)
        S.barrier()
        ck2(103)

        ya2B = unit3(A_UDE); ya2_b = [Buf() for _ in range(KT)]
        for m in range(M2):
            bks = linear(l, "glu", m, lambda kt: (yaB[:, kt, 0:N], ya_b[kt]), KT, N)
            for mi in range(2):
                kt = 2 * m + mi
                sg, sgb = tb()
                S.op("act", "activation", [bks[mi][1], PVb[l]], [sgb], out=sg[:, 0:N], in_=bks[mi][0][:, 0:N],
                     func=AF.Sigmoid, bias=PV[l][:, VI["b_glu"], kt:kt + 1], scale=1.0)
                S.op("pool", "tensor_tensor", [sgb, ya_b[kt]], [ya2_b[kt]], out=ya2B[:, kt, 0:N], in0=yaB[:, kt, 0:N],
                     in1=sg[:, 0:N], op=ALU.mult)

        ck2(104)
        ybB = unit3(A_S5); yb_b = [Buf() for _ in range(KT)]
        ET = [V(A_S5 + UNIT + i * 2 * NMAX, NMAX, F32) for i in range(NET)]
        ET_b = [Buf() for _ in range(NET)]
        for b in range(NB):
            bx = linear(l, "win", M2 + b, hrhs(N), KT, N)
            xcbf = []
            for mi in range(2):
                kt = 2 * b + mi
                xp, xpb = xpad[mi], xpad_b[mi]
                xc, xcb = ET[mi], ET_b[mi]
                pb, pbb = bx[mi]
                pv = PV[l]
                for si, seg in enumerate(segs):
                    c0, L = seg["c0"], seg["L"]
                    off = c0 + 3 * si
                    cidx = 0 if seg["kind"] == "p" else 1 + seg["idx"]
                    S.op("pool", "tensor_copy", [CONVcar_b[l]], [xpb], out=xp[:, off:off + 3], in_=CONVcar[l][:, cidx, :, kt])
                    S.op("act", "copy", [pbb], [xpb], out=xp[:, off + 3:off + 3 + L], in_=pb[:, c0:c0 + L])
                    S.op("pool", "tensor_copy", [xpb], [CONVcar_b[l]], out=CONVcar[l][:, cidx, :, kt], in_=xp[:, off + L:off + L + 3])
                    S.op("dve", "tensor_scalar", [xpb, PVb[l]], [xcb], out=xc[:, c0:c0 + L], in0=xp[:, off + 3:off + 3 + L],
                         scalar1=pv[:, VI["conv_w3"], kt:kt + 1], scalar2=pv[:, VI["conv_b"], kt:kt + 1], op0=ALU.mult,
                         op1=ALU.add)
                    for k in range(3):
                        S.op("dve", "scalar_tensor_tensor", [xpb, xcb, PVb[l]], [xcb], out=xc[:, c0:c0 + L],
                             in0=xp[:, off + k:off + k + L], scalar=pv[:, VI["conv_w%d" % k], kt:kt + 1], in1=xc[:, c0:c0 + L],
                             op0=ALU.mult, op1=ALU.add)
                xh, xhb = tb()
                S.op("pool", "tensor_copy", [xcb], [xhb], out=xh[:, 0:N], in_=xc[:, 0:N])
                xcbf.append((xh, xhb))
            slr, slrb, _ = w_get(l, "wr", 0, b)
            sli, slib, _ = w_get(l, "wi", 0, b)
            wr3 = slr[:, 0:512].rearrange("p (k c) -> p k c", k=2)
            wi3 = sli[:, 0:512].rearrange("p (k c) -> p k c", k=2)
            gate_banks = []
            for mo in range(2):
                rb_, rbb = bank(); ib_, ibb = bank()
                for (w3, wb_, ob, obb) in [(wr3, slrb, rb_, rbb), (wi3, slib, ib_, ibb)]:
                    for k2 in range(2):
                        S.op("pe", "matmul", [wb_, xcbf[k2][1]], [obb], inc=(k2 == 1), out=ob[:, 0:N],
                             lhsT=w3[:, k2, mo * 128:(mo + 1) * 128], rhs=xcbf[k2][0][:, 0:N], start=(k2 == 0), stop=(k2 == 1))
                gate_banks.append((rb_, rbb, ib_, ibb))
            w_done(2)
            hs_list = []
            for mo in range(2):
                kt = 2 * b + mo
                rb_, rbb, ib_, ibb = gate_banks[mo]
                e0 = 2 + mo * 5
                rT, rTb = ET[e0], ET_b[e0]
                iT, iTb = ET[e0 + 1], ET_b[e0 + 1]
                a2T, a2Tb = ET[e0 + 2], ET_b[e0 + 2]
                hsT, hsTb = ET[e0 + 3], ET_b[e0 + 3]
                xc, xcb = ET[mo], ET_b[mo]
                S.op("act", "activation", [rbb, PVb[l]], [rTb], out=rT[:, 0:N], in_=rb_[:, 0:N], func=AF.Sigmoid,
                     bias=pv[:, VI["lru_b_r"], kt:kt + 1], scale=1.0)
                S.op("act", "activation", [ibb, PVb[l]], [iTb], out=iT[:, 0:N], in_=ib_[:, 0:N], func=AF.Sigmoid,
                     bias=pv[:, VI["lru_b_i"], kt:kt + 1], scale=1.0)
                S.op("act", "activation", [rTb, PVb[l]], [a2Tb], out=a2T[:, 0:N], in_=rT[:, 0:N], func=AF.Exp,
                     scale=C8[l][:, 1, kt:kt + 1])
                S.op("act", "activation", [rTb, PVb[l]], [rTb], out=rT[:, 0:N], in_=rT[:, 0:N], func=AF.Exp,
                     scale=C8[l][:, 0, kt:kt + 1])
                S.op("act", "activation", [a2Tb], [a2Tb], out=a2T[:, 0:N], in_=a2T[:, 0:N], func=AF.Sqrt, bias=1.0, scale=-1.0)
                S.op("dve", "tensor_tensor", [iTb, xcb], [iTb], out=iT[:, 0:N], in0=iT[:, 0:N], in1=xc[:, 0:N], op=ALU.mult)
                S.op("dve", "tensor_tensor", [iTb, a2Tb], [iTb], out=iT[:, 0:N], in0=iT[:, 0:N], in1=a2T[:, 0:N], op=ALU.mult)
                for seg in segs:
                    c0, L = seg["c0"], seg["L"]
                    cidx = 0 if seg["kind"] == "p" else 1 + seg["idx"]
                    S.op("dve", "tensor_tensor_scan", [rTb, iTb, LRUcar_b[l]], [hsTb], out=hsT[:, c0:c0 + L],
                         data0=rT[:, c0:c0 + L], data1=iT[:, c0:c0 + L], initial=LRUcar[l][:, cidx, kt:kt + 1],
                         op0=ALU.mult, op1=ALU.add)
                    S.op("pool", "tensor_copy", [hsTb], [LRUcar_b[l]], out=LRUcar[l][:, cidx, kt:kt + 1],
                         in_=hsT[:, c0 + L - 1:c0 + L])
                hs_list.append((hsT, hsTb))
            bg = linear(l, "win", 2 * M2 + b, hrhs(N), KT, N)
            for mi in range(2):
                kt = 2 * b + mi
                gg, ggb = ET[12 + mi], ET_b[12 + mi]
                S.op("act", "activation", [bg[mi][1]], [ggb], out=gg[:, 0:N], in_=bg[mi][0][:, 0:N], func=AF.Gelu_apprx_tanh)
                S.op("dve", "tensor_tensor", [ggb, hs_list[mi][1]], [yb_b[kt]], out=ybB[:, kt, 0:N], in0=hs_list[mi][0][:, 0:N],
                     in1=gg[:, 0:N], op=ALU.mult)
        S.barrier()

        ck2(105)
        mgB = unit3(A_S5 + UNIT); mg_b = [Buf() for _ in range(KT)]
        for m in range(M2):
            ga = linear(l, "win", 3 * M2 + m, hrhs(N), KT, N)
            gb = linear(l, "win", 4 * M2 + m, hrhs(N), KT, N)
            sgs = []
            for mi in range(2):
                sa, sab = tf(); sb_, sbb = tf()
                S.op("act", "activation", [ga[mi][1]], [sab], out=sa[:, 0:N], in_=ga[mi][0][:, 0:N], func=AF.Sigmoid)
                S.op("act", "activation", [gb[mi][1]], [sbb], out=sb_[:, 0:N], in_=gb[mi][0][:, 0:N], func=AF.Sigmoid)
                sgs.append((sa, sab, sb_, sbb))
            pa_ = linear(l, "pa", m, lambda kt: (ya2B[:, kt, 0:N], ya2_b[kt]), KT, N)
            pb_ = linear(l, "pb", m, lambda kt: (ybB[:, kt, 0:N], yb_b[kt]), KT, N)
            for mi in range(2):
                kt = 2 * m + mi
                sa, sab, sb_, sbb = sgs[mi]
                S.op("dve", "tensor_tensor", [sab, pa_[mi][1]], [sab], out=sa[:, 0:N], in0=sa[:, 0:N], in1=pa_[mi][0][:, 0:N],
                     op=ALU.mult)
                S.op("dve", "tensor_tensor", [sbb, pb_[mi][1]], [sbb], out=sb_[:, 0:N], in0=sb_[:, 0:N], in1=pb_[mi][0][:, 0:N],
                     op=ALU.mult)
                S.op("pool", "tensor_tensor", [sab, sbb], [mg_b[kt]], out=mgB[:, kt, 0:N], in0=sa[:, 0:N], in1=sb_[:, 0:N],
                     op=ALU.add)

        ck2(106)
        oB = unit3(A_S5 + 2 * UNIT); o_b = [Buf() for _ in range(KT)]
        for m in range(M2):
            ob = linear(l, "wout", m, lambda kt: (mgB[:, kt, 0:N], mg_b[kt]), KT, N)
            for mi in range(2):
                kt = 2 * m + mi
                evcopy(oB[:, kt, 0:N], ob[mi][0][:, 0:N], [ob[mi][1]], [o_b[kt]])
        for kt in range(KT):
            sumsq_acc(oB[:, kt, 0:N], o_b[kt], kt == 0, kt == KT - 1, N)
        ck2(107)
        residual(l, "g_post_mix", oB, o_b, N)
        ck2(108)
        S.barrier()

    def ffn(l, N):
        actB = V(AOFF, FT * NMAX).rearrange("p (k n) -> p k n", k=FT)
        act_b = [Buf() for _ in range(FT)]
        fB = unit3(AOFF + ARENA - UNIT); f_b = [Buf() for _ in range(KT)]
        prenorm(l, "g_pre_ffn", N)
        for f in range(F2):
            gt = linear(l, "gate", f, hrhs(N), KT, N)
            up = linear(l, "up", f, hrhs(N), KT, N)
            for mi in range(2):
                ft = 2 * f + mi
                sg, sgb = tf()
                S.op("act", "activation", [gt[mi][1]], [sgb], out=sg[:, 0:N], in_=gt[mi][0][:, 0:N], func=AF.Silu)
                S.op("dve", "tensor_tensor", [sgb, up[mi][1]], [act_b[ft]], out=actB[:, ft, 0:N], in0=sg[:, 0:N],
                     in1=up[mi][0][:, 0:N], op=ALU.mult)
        for m in range(M2):
            ob = linear(l, "down", m, lambda ft: (actB[:, ft, 0:N], act_b[ft]), FT, N)
            for mi in range(2):
                kt = 2 * m + mi
                evcopy(fB[:, kt, 0:N], ob[mi][0][:, 0:N], [ob[mi][1]], [f_b[kt]])
        for kt in range(KT):
            sumsq_acc(fB[:, kt, 0:N], f_b[kt], kt == 0, kt == KT - 1, N)
        residual(l, "g_post_ffn", fB, f_b, N)
        S.barrier()

    for ti in range(cfg.NT):
        last_tile = ti == cfg.NT - 1
        N = NMAX if last_tile else NP
        segs = [dict(kind="p", idx=0, c0=0, L=NP)]
        blocks = [(x_p, y_p, ti * NP + r * 128, 128, r * 128) for r in range(NP // 128)]
        if last_tile:
            for s in range(NS):
                segs.append(dict(kind="s", idx=s, c0=NP + s * DEC, L=DEC))
            blocks.append((x_s, y_s, 0, NSAMP, NP))
        xin = []
        for bi, (src, _, r0, rows, c0) in enumerate(blocks):
            st_ = V(AOFF + bi * 2 * D, D, F32); stb_ = Buf()
            S.dma("sp", st_[0:rows, :], src[r0:r0 + rows, :], writes=[stb_], arena=True)
            xin.append((st_, stb_))
        for kt in range(KT):
            pb, pbb = bank()
            for bi, (src, _, r0, rows, c0) in enumerate(blocks):
                S.op("pe", "transpose", [xin[bi][1], const_b], [pbb], inc=(bi == len(blocks) - 1), out=pb[:, c0:c0 + rows],
                     in_=xin[bi][0][0:rows, kt * 128:(kt + 1) * 128], identity=identF[0:rows, 0:rows])
            evcopy(xF[:, kt, 0:N], pb[:, 0:N], [pbb], [x_b[kt]])
        S.barrier()
        ck2(5 + 10 * ti)
        for l in range(DEPTH):
            mixer(l, N, segs)
            ck2(6 + 10 * ti + 2 * l)
            ffn(l, N)
            ck2(7 + 10 * ti + 2 * l)
        for bi, (_, dst, r0, rows, c0) in enumerate(blocks):
            st_ = V(AOFF + bi * 2 * D, D, F32); stb_ = Buf()
            for k4 in range(0, KT, 4):
                nk4 = min(4, KT - k4)
                pb, pbb = bank()
                for k in range(nk4):
                    S.op("pe", "transpose", [x_b[k4 + k], const_b], [pbb], inc=(k == nk4 - 1),
                         out=pb[0:rows, k * 128:(k + 1) * 128], in_=xF[:, k4 + k, c0:c0 + rows], identity=identF[:, :])
                evcopy(st_[0:rows, k4 * 128:(k4 + nk4) * 128], pb[0:rows, 0:nk4 * 128], [pbb], [stb_])
            S.dma("sp", dst[r0:r0 + rows, :], st_[0:rows, :], reads=[stb_], arena=True)
        S.barrier()

    for l in range(DEPTH):
        for ri, dst in enumerate([o_ps5re, o_ps5im]):
            transpose_to(iost[0:GQ, :], S5car[l][:, ri, :], 128, GQ, [S5car_b[l]], [iost_b])
            S.dma("sp", dst[l], iost[0:GQ, :], reads=[iost_b])
        transpose_to(iost[0:KT, :], LRUcar[l][:, 0, :], 128, KT, [LRUcar_b[l]], [iost_b])
        S.dma("sp", o_plru[l], iost[0:KT, :], reads=[iost_b])
        transpose_to(iost[0:3 * KT, :], CONVcar[l][:, 0, :, :].rearrange("p a k -> p (a k)"), 128, 3 * KT, [CONVcar_b[l]],
                     [iost_b])
        S.dma("sp", o_pconv[l], iost[0:3 * KT, :], reads=[iost_b])
        for s in range(NS):
            for ri, dst in enumerate([o_ss5re, o_ss5im]):
                transpose_to(iost[0:GQ, :], S5ini[l][s][:, ri, :], 128, GQ, [S5ini_b[l][s]], [iost_b])
                S.dma("sp", dst[l, s], iost[0:GQ, :], reads=[iost_b])
            transpose_to(iost[0:KT, :], LRUcar[l][:, 1 + s, :], 128, KT, [LRUcar_b[l]], [iost_b])
            S.dma("sp", o_slru[l, s], iost[0:KT, :], reads=[iost_b])
            transpose_to(iost[0:3 * KT, :], CONVcar[l][:, 1 + s, :, :].rearrange("p a k -> p (a k)"), 128, 3 * KT,
                         [CONVcar_b[l]], [iost_b])
            S.dma("sp", o_sconv[l, s], iost[0:3 * KT, :], reads=[iost_b])
    assert wst["use"] == len(seq) and wst["done"] == len(seq)
    finalize()


def make_in_maps(cfg, inp):
    n_cores = 8
    D, KT, G, GQ, NS, DEPTH = cfg.D, cfg.KT, cfg.G, cfg.GQ, cfg.NS, cfg.DEPTH
    f = lambda a: np.ascontiguousarray(np.asarray(a, dtype=np.float32))
    shared = {}
    for n in ["g_pre_mix", "s5_d", "b_glu", "conv_b", "lru_b_r", "lru_b_i", "lru_lam", "g_post_mix", "g_pre_ffn",
              "g_post_ffn"]:
        shared[n] = f(inp[n]).reshape(DEPTH, KT, 128)
    shared["conv_w"] = f(inp["conv_w"]).reshape(DEPTH, 4, KT, 128)
    for n in ["w_in", "w_glu", "p_a", "p_b", "w_out", "lru_w_r", "lru_w_i", "w_gate", "w_up", "w_down", "s5_b_re",
              "s5_b_im", "s5_c_re", "s5_c_im"]:
        shared[n] = f(inp[n])
    shared["s5_lam_re"] = f(inp["s5_lam_re"]).reshape(DEPTH, GQ, 128)
    shared["s5_lam_im"] = f(inp["s5_lam_im"]).reshape(DEPTH, GQ, 128)
    shared["s5_log_step"] = f(inp["s5_log_step"]).reshape(DEPTH, GQ, 2)
    maps = []
    for c in range(n_cores):
        m = dict(shared)
        m["x_p"] = f(inp["x_prompt"][c])
        m["x_s"] = f(inp["x_sample"][NS * c:NS * (c + 1)]).reshape(cfg.NSAMP, D)
        m["i_s5re"] = f(inp["state_s5_re"][:, NS * c:NS * (c + 1)]).reshape(DEPTH, NS, GQ, 128)
        m["i_s5im"] = f(inp["state_s5_im"][:, NS * c:NS * (c + 1)]).reshape(DEPTH, NS, GQ, 128)
        m["i_lru"] = f(inp["state_lru"][:, NS * c:NS * (c + 1)]).reshape(DEPTH, NS, KT, 128)
        m["i_conv"] = f(inp["cache_conv"][:, NS * c:NS * (c + 1)]).reshape(DEPTH, NS, 3 * KT, 128)
        maps.append(m)
    return maps


def gather(cfg, results):
    D, G, NS, DEPTH, SEQ, DEC = cfg.D, cfg.G, cfg.NS, cfg.DEPTH, cfg.SEQ, cfg.DEC_SEQ
    nc_ = len(results)
    yp = np.stack([r["y_p"] for r in results]).reshape(nc_, SEQ, D)
    ys = np.concatenate([r["y_s"].reshape(NS, DEC, D) for r in results], axis=0)
    pre = np.stack([r["o_ps5re"].reshape(DEPTH, G, P_STATE) for r in results], axis=1)
    pim = np.stack([r["o_ps5im"].reshape(DEPTH, G, P_STATE) for r in results], axis=1)
    plru = np.stack([r["o_plru"].reshape(DEPTH, D) for r in results], axis=1)
    pconv = np.stack([r["o_pconv"].reshape(DEPTH, 3, D) for r in results], axis=1)
    sre = np.concatenate([r["o_ss5re"].reshape(DEPTH, NS, G, P_STATE) for r in results], axis=1)
    sim = np.concatenate([r["o_ss5im"].reshape(DEPTH, NS, G, P_STATE) for r in results], axis=1)
    slru = np.concatenate([r["o_slru"].reshape(DEPTH, NS, D) for r in results], axis=1)
    sconv = np.concatenate([r["o_sconv"].reshape(DEPTH, NS, 3, D) for r in results], axis=1)
    outs = (yp, ys, pre, pim, plru, pconv, sre, sim, slru, sconv)
    return tuple(np.ascontiguousarray(o.astype(np.float32)) for o in outs)


def run(cfg, inp):
    nc, _ = build(cfg)
    maps = make_in_maps(cfg, inp)
    res = run_bass_kernel_spmd(nc, maps, core_ids=list(range(8)))
    return gather(cfg, res.results)


def kernel(**inputs):
    return run(Cfg(), inputs)
```

```python
import math
from contextlib import ExitStack
import numpy as np
import concourse.bass as bass
import concourse.mybir as mybir
from concourse.bass_utils import run_bass_kernel_spmd

F32 = mybir.dt.float32
BF16 = mybir.dt.bfloat16
AF = mybir.ActivationFunctionType
ALU = mybir.AluOpType
EPOCH = 16000
NDSEM = 28
P_STATE = 64
HG = 16
T = 4
EPS = 1e-6


class Cfg:
    def __init__(s, D=4096, SEQ=2048, DEC_SEQ=16, NS=2, DEPTH=2, NP=256, DFF=11008, KC=16, NSLOT=3, QJ=8):
        s.D = D; s.SEQ = SEQ; s.DEC_SEQ = DEC_SEQ; s.NS = NS; s.DEPTH = DEPTH; s.NP = NP; s.DFF = DFF
        s.KT = D // 128; s.FT = DFF // 128; s.M2 = D // 256; s.F2 = DFF // 256
        s.G = D // HG; s.GQ = s.G // 2; s.NB = D // 256
        s.KC = min(KC, s.KT); s.NSLOT = NSLOT
        s.NT = SEQ // NP; s.NSAMP = NS * DEC_SEQ; s.NMAX = NP + s.NSAMP
        s.QJ = min(QJ, s.KT); s.NQ = s.KT // s.QJ
        s.NCHMAX = s.NMAX // T
        assert SEQ % NP == 0 and NP % 128 == 0 and DEC_SEQ % T == 0 and s.NSAMP <= 128
        assert DFF % 256 == 0 and D % 256 == 0


class Buf:
    __slots__ = ("name", "w", "r", "region", "wd")

    def __init__(self, name="", region=False):
        self.name = name; self.w = None; self.r = {}; self.region = region; self.wd = {}


class Sched:
    ENG = ["pe", "act", "dve", "pool", "sp"]

    def __init__(self):
        self.ops = {e: [] for e in self.ENG}
        self.cnt = {e: 0 for e in self.ENG}
        self.known = {e: {} for e in self.ENG}
        self.dma_tot = [0] * NDSEM
        self.dma_rr = 0
        self.arena_dma = []

    def _need(self, eng, key, val):
        k = self.known[eng]
        if k.get(key, 0) >= val:
            return
        k[key] = val
        self.ops[eng].append(("wait", key, val))

    def _dep(self, eng, tok, war=False):
        key, val, src = tok
        if src == eng and eng == "pe":
            return
        self._need(eng, key, val)

    def _deps(self, eng, reads, writes):
        for b in reads:
            if b.w is not None:
                self._dep(eng, b.w)
            for t in b.wd.values():
                self._dep(eng, t)
        for b in writes:
            if b.region:
                continue
            if b.w is not None:
                self._dep(eng, b.w)
            for t in b.r.values():
                self._dep(eng, t, war=True)

    def op(self, eng, method, reads=(), writes=(), inc=True, **kw):
        self._deps(eng, reads, writes)
        idx = self.cnt[eng]
        key = (eng, idx // EPOCH); val = idx % EPOCH + 1
        if inc:
            self.cnt[eng] += 1
        self.ops[eng].append(("op", method, kw, key if inc else None))
        tok = (key, val, eng)
        for b in reads:
            b.r[eng] = tok
        for b in writes:
            b.w = tok; b.r = {}
        return tok

    def dma(self, eng, out, in_, reads=(), writes=(), arena=False, **kw):
        s = self.dma_rr; self.dma_rr = (s + 1) % NDSEM
        if self.dma_tot[s] > 0:
            self._need(eng, ("d", s), self.dma_tot[s])
        if arena:
            for (key, val) in getattr(self, "bar_toks", []):
                self._need(eng, key, val)
        self._deps(eng, reads, writes)
        self.dma_tot[s] += 16
        self.ops[eng].append(("dma", out, in_, kw, s))
        tok = (("d", s), self.dma_tot[s], "dma")
        for b in reads:
            b.r[("d", s)] = tok
        for b in writes:
            if b.region:
                b.wd[("d", s)] = tok
            else:
                b.w = tok; b.r = {}
        if arena:
            self.arena_dma.append(tok)
        return tok

    def barrier(self):
        ce = ["pe", "act", "dve", "pool"]
        toks = []
        for e in ce:
            if self.cnt[e] > 0:
                idx = self.cnt[e] - 1
                toks.append((e, ((e, idx // EPOCH), idx % EPOCH + 1)))
        for e in ce:
            for (src, (key, val)) in toks:
                if src != e:
                    self._need(e, key, val)
            for (key, val, _) in self.arena_dma:
                self._need(e, key, val)
        self.bar_toks = [kv for (_, kv) in toks] + [(key, val) for (key, val, _) in self.arena_dma]
        self.arena_dma = []

    def finish(self):
        for s in range(NDSEM):
            if self.dma_tot[s] > 0:
                self._need("sp", ("d", s), self.dma_tot[s])
        ce = ["pe", "act", "dve", "pool"]
        for e in ce:
            if self.cnt[e] > 0:
                idx = self.cnt[e] - 1
                self._need("sp", (e, idx // EPOCH), idx % EPOCH + 1)

    def sem_keys(self):
        keys = set()
        for e in self.ENG:
            for o in self.ops[e]:
                if o[0] == "wait":
                    keys.add(o[1])
                elif o[0] == "op" and o[3] is not None:
                    keys.add(o[3])
                elif o[0] == "dma":
                    keys.add(("d", o[4]))
        return sorted(keys, key=str)


def dram_ap(t, offset, ap):
    return bass.AP(t.tensor, offset, ap)


def wregions(cfg):
    regs = {}
    sizes = {}

    def add(key, nk, count):
        off = sizes.get(key[0], 0)
        regs[key] = (off, nk, count)
        sizes[key[0]] = off + count * 128 * nk * 256

    D, DFF, KC = cfg.D, cfg.DFF, cfg.KC
    for name, K, M in [("win", D, 5 * D), ("glu", D, D), ("pa", D, D), ("pb", D, D), ("wout", D, D),
                       ("gate", D, DFF), ("up", D, DFF), ("down", DFF, D)]:
        kts = K // 128
        for kc in range((kts + KC - 1) // KC):
            add((name, kc), min(KC, kts - kc * KC), M // 256)
    add(("wr", 0), 2, cfg.NB); add(("wi", 0), 2, cfg.NB)
    add(("wx", 0), 16, cfg.KT); add(("wy", 0), 16, cfg.KT); add(("wk", 0), 2, cfg.KT)
    return regs, sizes


def weight_order(cfg):
    o = []
    nkc = (cfg.KT + cfg.KC - 1) // cfg.KC
    fkc = (cfg.FT + cfg.KC - 1) // cfg.KC
    M2 = cfg.M2

    def lin(name, m, n=nkc):
        for kc in range(n):
            o.append((name, kc, m))
    for qq in range(cfg.NQ):
        for j in range(qq * cfg.QJ, (qq + 1) * cfg.QJ):
            if j % 2 == 0:
                lin("win", j // 2)
            o.append(("wx", 0, j))
        if qq >= 1:
            for j in range((qq - 1) * cfg.QJ, qq * cfg.QJ):
                o.append(("wk", 0, j)); o.append(("wy", 0, j))
    for j in range((cfg.NQ - 1) * cfg.QJ, cfg.NQ * cfg.QJ):
        o.append(("wk", 0, j)); o.append(("wy", 0, j))
    for m in range(M2):
        lin("glu", m)
    for b in range(cfg.NB):
        lin("win", M2 + b); o.append(("wr", 0, b)); o.append(("wi", 0, b)); lin("win", 2 * M2 + b)
    for m in range(M2):
        lin("win", 3 * M2 + m); lin("win", 4 * M2 + m); lin("pa", m); lin("pb", m)
    for m in range(M2):
        lin("wout", m)
    for f in range(cfg.F2):
        lin("gate", f); lin("up", f)
    for m in range(M2):
        lin("down", m, fkc)
    return o


VEC_NAMES = ["g_pre_mix", "s5_d", "b_glu", "conv_w0", "conv_w1", "conv_w2", "conv_w3", "conv_b",
             "lru_b_r", "lru_b_i", "lru_lam", "g_post_mix", "g_pre_ffn", "g_post_ffn"]
VI = {n: i for i, n in enumerate(VEC_NAMES)}
NV = len(VEC_NAMES)


class _Stop(Exception):
    pass


def build(cfg):
    import os
    nc = bass.Bass("TRN2", target_bir_lowering=False)
    S = Sched()
    kstop = int(os.environ.get("KSTOP", "0"))

    def ck(n):
        if kstop == n:
            raise _Stop()
    try:
        _build_body(nc, S, cfg, ck)
    except _Stop:
        pass
    return nc, S


def _build_body(nc, S, cfg, ck):
    D, KT, FT, M2, F2, G, GQ, NB = cfg.D, cfg.KT, cfg.FT, cfg.M2, cfg.F2, cfg.G, cfg.GQ, cfg.NB
    DEPTH, NS, NMAX, NP, KC, QJ, NQ = cfg.DEPTH, cfg.NS, cfg.NMAX, cfg.NP, cfg.KC, cfg.QJ, cfg.NQ
    DFF, SEQ, NSAMP, DEC = cfg.DFF, cfg.SEQ, cfg.NSAMP, cfg.DEC_SEQ
    NCHMAX = cfg.NCHMAX
    WQ = QJ * 4

    def din(name, shape):
        return nc.dram_tensor(name, list(shape), F32, kind="ExternalInput").ap()

    def dout(name, shape):
        return nc.dram_tensor(name, list(shape), F32, kind="ExternalOutput").ap()

    x_p = din("x_p", [SEQ, D]); x_s = din("x_s", [NSAMP, D])
    i_s5re = din("i_s5re", [DEPTH, NS, GQ, 128]); i_s5im = din("i_s5im", [DEPTH, NS, GQ, 128])
    i_lru = din("i_lru", [DEPTH, NS, KT, 128]); i_conv = din("i_conv", [DEPTH, NS, 3 * KT, 128])
    vec_in = {}
    for n in ["g_pre_mix", "s5_d", "b_glu", "conv_b", "lru_b_r", "lru_b_i", "lru_lam", "g_post_mix",
              "g_pre_ffn", "g_post_ffn"]:
        vec_in[n] = din(n, [DEPTH, KT, 128])
    conv_w_in = din("conv_w", [DEPTH, 4, KT, 128])
    w_in = din("w_in", [DEPTH, D, 5 * D])
    lam_re = din("s5_lam_re", [DEPTH, GQ, 128]); lam_im = din("s5_lam_im", [DEPTH, GQ, 128])
    log_step = din("s5_log_step", [DEPTH, GQ, 2])
    b_re = din("s5_b_re", [DEPTH, G, P_STATE, HG]); b_im = din("s5_b_im", [DEPTH, G, P_STATE, HG])
    c_re = din("s5_c_re", [DEPTH, G, HG, P_STATE]); c_im = din("s5_c_im", [DEPTH, G, HG, P_STATE])
    w_glu = din("w_glu", [DEPTH, D, D]); p_a = din("p_a", [DEPTH, D, D]); p_b = din("p_b", [DEPTH, D, D])
    w_out = din("w_out", [DEPTH, D, D])
    lru_w_r = din("lru_w_r", [DEPTH, NB, 256, 256]); lru_w_i = din("lru_w_i", [DEPTH, NB, 256, 256])
    w_gate = din("w_gate", [DEPTH, D, DFF]); w_up = din("w_up", [DEPTH, D, DFF]); w_down = din("w_down", [DEPTH, DFF, D])

    y_p = dout("y_p", [SEQ, D]); y_s = dout("y_s", [NSAMP, D])
    o_ps5re = dout("o_ps5re", [DEPTH, GQ, 128]); o_ps5im = dout("o_ps5im", [DEPTH, GQ, 128])
    o_plru = dout("o_plru", [DEPTH, KT, 128]); o_pconv = dout("o_pconv", [DEPTH, 3 * KT, 128])
    o_ss5re = dout("o_ss5re", [DEPTH, NS, GQ, 128]); o_ss5im = dout("o_ss5im", [DEPTH, NS, GQ, 128])
    o_slru = dout("o_slru", [DEPTH, NS, KT, 128]); o_sconv = dout("o_sconv", [DEPTH, NS, 3 * KT, 128])

    regs, wsizes = wregions(cfg)
    wscr = [{n: nc.dram_tensor("wscr%d_%s" % (l, n), [sz], BF16, kind="Internal").ap() for n, sz in wsizes.items()}
            for l in range(DEPTH)]

    es = ExitStack()

    def finalize():
        S.finish()
        keys = S.sem_keys()
        sems = {k: es.enter_context(nc.semaphore("s%d" % i)) for i, k in enumerate(keys)}
        block = es.enter_context(nc.Block())

        def make_body(ops):
            def body(eng):
                for o in ops:
                    if o[0] == "wait":
                        eng.wait_ge(sems[o[1]], o[2])
                    elif o[0] == "op":
                        ins = getattr(eng, o[1])(**o[2])
                        if o[3] is not None:
                            ins.then_inc(sems[o[3]], 1)
                    else:
                        eng.dma_start(out=o[1], in_=o[2], **o[3]).then_inc(sems[("d", o[4])], 16)
            return body

        block.sync(make_body(S.ops["sp"]))
        block.tensor(make_body(S.ops["pe"]))
        block.scalar(make_body(S.ops["act"]))
        block.vector(make_body(S.ops["dve"]))
        block.gpsimd(make_body(S.ops["pool"]))
        es.close()

    def ck2(n):
        try:
            ck(n)
        except _Stop:
            finalize()
            raise


    def sbt(name, shape, dt=F32):
        return es.enter_context(nc.sbuf_tensor(name, list(shape), dt))

    UNIT = KT * NMAX
    XOFF = 0
    HOFF = XOFF + 2 * UNIT
    AOFF = HOFF + UNIT
    need_mid = max(12 * QJ * NMAX, UNIT + 32 * NMAX, 3 * UNIT)
    need = max(2 * UNIT + need_mid, UNIT + FT * NMAX, (NP // 128 + 1) * 2 * D)
    NUNIT = (need + UNIT - 1) // UNIT
    BIGN = max(AOFF + NUNIT * UNIT, 24576)
    big = sbt("big", [128, BIGN], BF16)

    def V(off, n, dt=BF16):
        if dt == BF16:
            return big[:, off:off + n]
        return big[:, off:off + 2 * n].bitcast(F32)

    def U(i):
        return AOFF + i * UNIT

    SLOTN = 16 * 256
    wslots = [sbt("wslot%d" % i, [128, SLOTN], BF16) for i in range(cfg.NSLOT)]
    wslot_buf = [Buf("ws%d" % i) for i in range(cfg.NSLOT)]
    NTF, NTB = 10, 6
    tmpf = [sbt("tmpf%d" % i, [128, NMAX], F32) for i in range(NTF)]
    tmpf_b = [Buf() for _ in range(NTF)]
    tmpb = [sbt("tmpb%d" % i, [128, NMAX], BF16) for i in range(NTB)]
    tmpb_b = [Buf() for _ in range(NTB)]
    rr = {"f": 0, "b": 0, "ps": 0, "ev": 0}

    def tf():
        i = rr["f"]; rr["f"] = (i + 1) % NTF
        return tmpf[i], tmpf_b[i]

    def tb():
        i = rr["b"]; rr["b"] = (i + 1) % NTB
        return tmpb[i], tmpb_b[i]

    psum = [es.enter_context(nc.psum_tensor("ps%d" % i, [128, 512], F32)) for i in range(8)]
    psum_b = [Buf("ps%d" % i) for i in range(8)]
    SSB = 7

    def bank():
        i = rr["ps"]; rr["ps"] = (i + 1) % 7
        return psum[i], psum_b[i]

    identF = sbt("identF", [128, 128], F32)
    onesD = sbt("onesD", [128, 128], BF16)
    maskbd = sbt("maskbd", [128, 128], F32)
    rstd = sbt("rstd", [128, NMAX], F32); rstd_b = Buf("rstd")
    PV = [sbt("pv%d" % l, [128, NV, KT], F32) for l in range(DEPTH)]
    PVb = [Buf("pv%d" % l) for l in range(DEPTH)]
    C8 = [sbt("c8_%d" % l, [128, 2, KT], F32) for l in range(DEPTH)]
    A4 = [sbt("a4_%d" % l, [128, 2, GQ], F32) for l in range(DEPTH)]
    A4b = [Buf() for _ in range(DEPTH)]
    S5car = [sbt("s5car%d" % l, [128, 2, GQ], F32) for l in range(DEPTH)]
    S5car_b = [Buf() for _ in range(DEPTH)]
    S5ini = [[sbt("s5ini%d_%d" % (l, s), [128, 2, GQ], F32) for s in range(NS)] for l in range(DEPTH)]
    S5ini_b = [[Buf() for _ in range(NS)] for l in range(DEPTH)]
    LRUcar = [sbt("lrucar%d" % l, [128, 1 + NS, KT], F32) for l in range(DEPTH)]
    LRUcar_b = [Buf() for _ in range(DEPTH)]
    CONVcar = [sbt("convcar%d" % l, [128, 1 + NS, 3, KT], F32) for l in range(DEPTH)]
    CONVcar_b = [Buf() for _ in range(DEPTH)]
    xpad = [sbt("xpad%d" % i, [128, NMAX + 3 * (1 + NS)], F32) for i in range(2)]
    xpad_b = [Buf() for _ in range(2)]
    sct = [sbt("sct%d" % i, [128, WQ], F32) for i in range(8)]
    sct_b = [Buf() for _ in range(8)]
    iost = sbt("iost", [128, 128], F32); iost_b = Buf()

    const_b = Buf("const")
    I32 = mybir.dt.int32
    ri32 = sbt("ri32", [128, 128], I32); rpi32 = sbt("rpi32", [128, 1], I32)
    rf = sbt("rf", [128, 128], F32); rp = sbt("rp", [128, 1], F32)
    S.op("pool", "memset", writes=[const_b], ap=onesD[:], constant=1.0 / D)
    S.op("pool", "iota", writes=[const_b], out=ri32[:], pattern=[[1, 128]], base=0, channel_multiplier=0)
    S.op("pool", "iota", writes=[const_b], out=rpi32[:], pattern=[[0, 1]], base=0, channel_multiplier=1)
    S.op("dve", "tensor_copy", [const_b], [const_b], out=rf[:], in_=ri32[:])
    S.op("dve", "tensor_copy", [const_b], [const_b], out=rp[:], in_=rpi32[:])
    S.op("dve", "tensor_scalar", [const_b], [const_b], out=identF[:], in0=rf[:], scalar1=rp[:, 0:1], scalar2=None,
         op0=ALU.is_equal)
    S.op("dve", "tensor_single_scalar", [const_b], [const_b], out=ri32[:], in_=ri32[:], scalar=4,
         op=ALU.arith_shift_right)
    S.op("dve", "tensor_single_scalar", [const_b], [const_b], out=rpi32[:], in_=rpi32[:], scalar=4,
         op=ALU.arith_shift_right)
    S.op("dve", "tensor_copy", [const_b], [const_b], out=rf[:], in_=ri32[:])
    S.op("dve", "tensor_copy", [const_b], [const_b], out=rp[:], in_=rpi32[:])
    S.op("dve", "tensor_scalar", [const_b], [const_b], out=maskbd[:], in0=rf[:], scalar1=rp[:, 0:1], scalar2=None,
         op0=ALU.is_equal)

    ck2(1)
    def evcopy(out, in_, reads, writes, eng=None):
        if eng is None:
            eng = ("act", "dve")[rr["ev"] % 2]; rr["ev"] += 1
        if eng == "act":
            S.op("act", "copy", reads, writes, out=out, in_=in_)
        else:
            S.op(eng, "tensor_copy", reads, writes, out=out, in_=in_)

    def transpose_to(out_sb, in_sb, rows, cols, reads, writes, eng=None):
        pb, pbb = bank()
        S.op("pe", "transpose", reads=list(reads) + [const_b], writes=[pbb], out=pb[0:cols, 0:rows], in_=in_sb,
             identity=identF[0:rows, 0:rows])
        evcopy(out_sb, pb[0:cols, 0:rows], [pbb], writes, eng)

    vstage = sbt("vstage", [128, 128], F32); vstage_b = Buf()
    for l in range(DEPTH):
        rows_total = NV * KT
        srcs = []
        for n in VEC_NAMES:
            if n.startswith("conv_w"):
                srcs.append(conv_w_in[l, int(n[-1])])
            else:
                srcs.append(vec_in[n][l])
        r0 = 0
        while r0 < rows_total:
            nr = min(128, rows_total - r0)
            r = r0
            while r < r0 + nr:
                vi, k0 = divmod(r, KT)
                cnt = min(KT - k0, r0 + nr - r)
                S.dma("sp", vstage[r - r0:r - r0 + cnt, :], srcs[vi][k0:k0 + cnt, :], writes=[vstage_b])
                r += cnt
            outv = PV[l][:].rearrange("p v k -> p (v k)")[:, r0:r0 + nr]
            transpose_to(outv, vstage[0:nr, :], nr, 128, [vstage_b], [PVb[l]])
            r0 += nr
        lamv = PV[l][:, VI["lru_lam"], :]
        t0, t0b = tf()
        S.op("act", "activation", [PVb[l]], [t0b], out=t0[:, 0:KT], in_=lamv, func=AF.Exp, scale=-1.0)
        S.op("act", "activation", [t0b], [t0b], out=t0[:, 0:KT], in_=t0[:, 0:KT], func=AF.Ln, bias=1.0, scale=1.0)
        S.op("dve", "tensor_scalar", [t0b], [PVb[l]], out=C8[l][:, 0, :], in0=t0[:, 0:KT], scalar1=-8.0, scalar2=None,
             op0=ALU.mult)
        S.op("dve", "tensor_scalar", [t0b], [PVb[l]], out=C8[l][:, 1, :], in0=t0[:, 0:KT], scalar1=-16.0, scalar2=None,
             op0=ALU.mult)
        S.op("pool", "memset", writes=[S5car_b[l]], ap=S5car[l][:], constant=0.0)
        S.op("pool", "memset", writes=[LRUcar_b[l]], ap=LRUcar[l][:], constant=0.0)
        S.op("pool", "memset", writes=[CONVcar_b[l]], ap=CONVcar[l][:], constant=0.0)
        for s in range(NS):
            for ri, src in enumerate([i_s5re, i_s5im]):
                S.dma("sp", vstage[0:GQ, :], src[l, s], writes=[vstage_b])
                transpose_to(S5ini[l][s][:, ri, :], vstage[0:GQ, :], GQ, 128, [vstage_b], [S5ini_b[l][s]])
            S.dma("sp", vstage[0:KT, :], i_lru[l, s], writes=[vstage_b])
            transpose_to(LRUcar[l][:, 1 + s, :], vstage[0:KT, :], KT, 128, [vstage_b], [LRUcar_b[l]])
            S.dma("sp", vstage[0:3 * KT, :], i_conv[l, s], writes=[vstage_b])
            transpose_to(CONVcar[l][:, 1 + s, :, :].rearrange("p a k -> p (a k)"), vstage[0:3 * KT, :], 3 * KT, 128,
                         [vstage_b], [CONVcar_b[l]])

    ck2(2)
    CW = 2048
    NST = 4
    LOOKAHEAD = NST - 1
    pst_f = [V(XOFF + i * 2 * CW, CW, F32) for i in range(NST)]
    pst_fb = [Buf() for _ in range(NST)]
    pst_h = [V(XOFF + NST * 2 * CW + i * CW, CW, BF16) for i in range(NST)]
    pst_hb = [Buf() for _ in range(NST)]
    assert NST * 3 * CW <= BIGN, "prepass staging does not fit"
    wreg_b = [Buf("wreg%d" % l, region=True) for l in range(DEPTH)]
    pieces = []
    for l in range(DEPTH):
        for name, src, K, M in [("win", w_in, D, 5 * D), ("glu", w_glu, D, D), ("pa", p_a, D, D), ("pb", p_b, D, D),
                                ("wout", w_out, D, D), ("gate", w_gate, D, DFF), ("up", w_up, D, DFF),
                                ("down", w_down, DFF, D)]:
            for kt in range(K // 128):
                base, nk, cnt = regs[(name, kt // KC)]
                CH = 128 * nk * 256
                c0 = 0
                while c0 < M:
                    cw = min(CW, M - c0)
                    dst = dram_ap(wscr[l][name], base + (c0 // 256) * CH + (kt % KC) * 256,
                                  [[nk * 256, 128], [CH, cw // 256], [1, 256]])
                    pieces.append((l, src[l, kt * 128:(kt + 1) * 128, c0:c0 + cw], cw, dst))
                    c0 += cw
        for name, src in [("wr", lru_w_r), ("wi", lru_w_i)]:
            base, nk, cnt = regs[(name, 0)]
            b0 = 0
            while b0 < NB:
                nb = min(4, NB - b0)
                srcv = [src[l, b0 + bb].rearrange("(k p) c -> p k c", p=128) for bb in range(nb)]
                dst = dram_ap(wscr[l][name], base + b0 * 128 * 512, [[512, 128], [128 * 512, nb], [1, 512]])
                pieces.append((l, srcv, nb * 512, dst))
                b0 += nb

    def pp_load(i):
        l, src_ap, ncols, dst_ap = pieces[i]
        sf, sfb = pst_f[i % NST], pst_fb[i % NST]
        if isinstance(src_ap, list):
            for bb, sv in enumerate(src_ap):
                S.dma("sp", sf[:, bb * 512:(bb + 1) * 512].rearrange("p (k c) -> p k c", k=2), sv, writes=[sfb])
        else:
            S.dma("sp", sf[:, 0:ncols], src_ap, writes=[sfb])

    for i in range(min(LOOKAHEAD, len(pieces))):
        pp_load(i)
    for i in range(len(pieces)):
        l, src_ap, ncols, dst_ap = pieces[i]
        sf, sfb = pst_f[i % NST], pst_fb[i % NST]
        sh, shb = pst_h[i % NST], pst_hb[i % NST]
        eng = ("dve", "act", "dve", "pool")[i % 4]
        evcopy(sh[:, 0:ncols], sf[:, 0:ncols], [sfb], [shb], eng)
        if i + LOOKAHEAD < len(pieces):
            pp_load(i + LOOKAHEAD)
        S.dma("sp", dst_ap, sh[:, 0:ncols].rearrange("p (m c) -> p m c", c=dst_ap.shape[-1]), reads=[shb],
              writes=[wreg_b[l]], arena=True)
    S.barrier()

    ck2(3)
    NGH = GQ * HG

    def s5_prep(l):
        off = [0]

        def alloc(n, dt=F32):
            o = off[0]; off[0] += n * (2 if dt == F32 else 1)
            assert off[0] <= BIGN, "s5 prep scratch overflow"
            return V(o, n, dt), Buf()

        def sm():
            return alloc(GQ)

        def v3(a):
            return a.rearrange("p (g h) -> p g h", h=HG)

        def bc(a):
            return a.unsqueeze(2).broadcast_to([128, GQ, HG])

        u1f, u1b = alloc(NGH); u2f, u2b = alloc(NGH)

        def cmul(o, a, b, negI=False, three=False):
            oR, oRb, oI, oIb = o; aR, aRb, aI, aIb = a; bR, bRb, bI, bIb = b
            if three:
                u1 = v3(u1f); u2 = v3(u2f)
            else:
                u1 = u1f[:, 0:GQ]; u2 = u2f[:, 0:GQ]
            S.op("dve", "tensor_tensor", [aRb, bRb], [u1b], out=u1, in0=aR, in1=bR, op=ALU.mult)
            S.op("dve", "tensor_tensor", [aIb, bIb], [u2b], out=u2, in0=aI, in1=bI, op=ALU.mult)
            S.op("dve", "tensor_tensor", [u1b, u2b], [oRb], out=oR, in0=u1, in1=u2, op=ALU.subtract)
            S.op("dve", "tensor_tensor", [aRb, bIb], [u1b], out=u1, in0=aR, in1=bI, op=ALU.mult)
            S.op("dve", "tensor_tensor", [aIb, bRb], [u2b], out=u2, in0=aI, in1=bR, op=ALU.mult)
            if negI:
                S.op("dve", "scalar_tensor_tensor", [u1b, u2b], [oIb], out=oI, in0=u1, scalar=-1.0, in1=u2,
                     op0=ALU.mult, op1=ALU.subtract)
            else:
                S.op("dve", "tensor_tensor", [u1b, u2b], [oIb], out=oI, in0=u1, in1=u2, op=ALU.add)

        tin, tinb = alloc(3 * 128)
        tin3 = tin.rearrange("p (a c) -> p a c", a=3)
        ls2, ls2b = alloc(2)
        S.dma("sp", tin3[0:GQ, 0, :], lam_re[l], writes=[tinb], arena=True)
        S.dma("sp", tin3[0:GQ, 1, :], lam_im[l], writes=[tinb], arena=True)
        S.dma("sp", ls2[0:GQ, :], log_step[l], writes=[ls2b], arena=True)
        S.op("dve", "tensor_copy", [ls2b, tinb], [tinb], out=tin3[0:GQ, 2, :].rearrange("p (a c) -> p a c", a=2),
             in_=ls2[0:GQ, :].unsqueeze(2).broadcast_to([GQ, 2, 64]))
        lamR, lamRb = sm(); lamI, lamIb = sm(); lst, lstb = sm()
        for a_, (dst, dstb) in enumerate([(lamR, lamRb), (lamI, lamIb), (lst, lstb)]):
            transpose_to(dst, tin3[0:GQ, a_, :], GQ, 128, [tinb], [dstb])
        st, stb = sm(); lrs, lrsb = sm(); ang, angb = sm()
        S.op("act", "activation", [lstb], [stb], out=st, in_=lst, func=AF.Exp)
        S.op("dve", "tensor_tensor", [stb, lamRb], [lrsb], out=lrs, in0=lamR, in1=st, op=ALU.mult)
        S.op("dve", "tensor_tensor", [stb, lamIb], [angb], out=ang, in0=lamI, in1=st, op=ALU.mult)
        mg, mgb = sm(); sn, snb = sm(); cs, csb = sm()
        S.op("act", "activation", [lrsb], [mgb], out=mg, in_=lrs, func=AF.Exp, scale=1.0 / 32)
        S.op("act", "activation", [angb], [snb], out=sn, in_=ang, func=AF.Sin, scale=1.0 / 32)
        halfpi, hpb = alloc(1)
        S.op("pool", "memset", writes=[hpb], ap=halfpi, constant=math.pi / 2)
        S.op("act", "activation", [angb, hpb], [csb], out=cs, in_=ang, func=AF.Sin, scale=1.0 / 32, bias=halfpi[:, 0:1])
        Ak = [None] * 5
        for k in range(1, 5):
            r_, rb_ = sm(); i_, ib_ = sm()
            Ak[k] = (r_, rb_, i_, ib_)
        t1, t1b = sm(); t2, t2b = sm()
        S.op("dve", "tensor_tensor", [mgb, csb], [Ak[1][1]], out=Ak[1][0], in0=mg, in1=cs, op=ALU.mult)
        S.op("dve", "tensor_tensor", [mgb, snb], [Ak[1][3]], out=Ak[1][2], in0=mg, in1=sn, op=ALU.mult)
        tt_ = (t1, t1b, t2, t2b)
        for it in range(5):
            cmul(tt_, Ak[1], Ak[1])
            S.op("dve", "tensor_copy", [t1b], [Ak[1][1]], out=Ak[1][0], in_=t1)
            S.op("dve", "tensor_copy", [t2b], [Ak[1][3]], out=Ak[1][2], in_=t2)
        cmul(Ak[2], Ak[1], Ak[1]); cmul(Ak[3], Ak[2], Ak[1]); cmul(Ak[4], Ak[2], Ak[2])
        S.op("pool", "tensor_copy", [Ak[4][1]], [A4b[l]], out=A4[l][:, 0, :], in_=Ak[4][0])
        S.op("pool", "tensor_copy", [Ak[4][3]], [A4b[l]], out=A4[l][:, 1, :], in_=Ak[4][2])
        nR, nRb = sm(); den, denb = sm(); cfR, cfRb = sm(); cfI, cfIb = sm()
        A1R, A1Rb, A1I, A1Ib = Ak[1]
        S.op("dve", "tensor_scalar", [A1Rb], [nRb], out=nR, in0=A1R, scalar1=-1.0, scalar2=None, op0=ALU.add)
        S.op("dve", "tensor_tensor", [lamRb], [t1b], out=t1, in0=lamR, in1=lamR, op=ALU.mult)
        S.op("dve", "tensor_tensor", [lamIb], [t2b], out=t2, in0=lamI, in1=lamI, op=ALU.mult)
        S.op("dve", "tensor_tensor", [t1b, t2b], [denb], out=den, in0=t1, in1=t2, op=ALU.add)
        S.op("dve", "reciprocal", [denb], [denb], out=den, in_=den)
        S.op("dve", "tensor_tensor", [nRb, lamRb], [t1b], out=t1, in0=nR, in1=lamR, op=ALU.mult)
        S.op("dve", "tensor_tensor", [A1Ib, lamIb], [t2b], out=t2, in0=A1I, in1=lamI, op=ALU.mult)
        S.op("dve", "tensor_tensor", [t1b, t2b], [t1b], out=t1, in0=t1, in1=t2, op=ALU.add)
        S.op("dve", "tensor_tensor", [t1b, denb], [cfRb], out=cfR, in0=t1, in1=den, op=ALU.mult)
        S.op("dve", "tensor_tensor", [A1Ib, lamRb], [t1b], out=t1, in0=A1I, in1=lamR, op=ALU.mult)
        S.op("dve", "tensor_tensor", [nRb, lamIb], [t2b], out=t2, in0=nR, in1=lamI, op=ALU.mult)
        S.op("dve", "tensor_tensor", [t1b, t2b], [t1b], out=t1, in0=t1, in1=t2, op=ALU.subtract)
        S.op("dve", "tensor_tensor", [t1b, denb], [cfIb], out=cfI, in0=t1, in1=den, op=ALU.mult)

        def bcA(k):
            r_, rb_, i_, ib_ = Ak[k]
            return (bc(r_), rb_, bc(i_), ib_)

        LZ = []
        for ri in range(2):
            lz, lzb = alloc(KT * 128, BF16)
            S.op("pool", "memset", writes=[lzb], ap=lz, constant=0.0)
            LZ.append((lz, lzb))
        mark = off[0]
        zt, ztb = alloc(4096, BF16)
        wxz_b = Buf()
        S.op("pool", "memset", writes=[ztb], ap=zt, constant=0.0)
        basex, _, _ = regs[("wx", 0)]
        for j in range(KT):
            dst = dram_ap(wscr[l]["wx"], basex + j * 128 * 4096, [[4096, 128], [1, 4096]])
            S.dma("sp", dst, zt, reads=[ztb], writes=[wreg_b[l], wxz_b], arena=True)
        BR, BRb = alloc(NGH); BI, BIb = alloc(NGH)
        for src, dst, dstb in [(b_re, BR, BRb), (b_im, BI, BIb)]:
            for g2 in range(2):
                srcv = dram_ap(src, l * G * P_STATE * HG + g2 * P_STATE * HG,
                               [[HG, P_STATE], [2 * P_STATE * HG, GQ], [1, HG]])
                S.dma("sp", v3(dst)[g2 * 64:(g2 + 1) * 64, :, :], srcv, writes=[dstb], arena=True)
        BbR, BbRb = alloc(NGH); BbI, BbIb = alloc(NGH)
        Bb3 = (v3(BbR), BbRb, v3(BbI), BbIb)
        cmul(Bb3, (bc(cfR), cfRb, bc(cfI), cfIb), (v3(BR), BRb, v3(BI), BIb), three=True)
        for ri, (src, srcb) in enumerate([(BbR, BbRb), (BbI, BbIb)]):
            lz5 = LZ[ri][0].rearrange("p (j q a h) -> p j q a h", j=KT, q=4, a=2)
            s4 = src.rearrange("p (j q h) -> p j q h", j=KT, q=4)
            for g2 in range(2):
                S.op("dve", "tensor_copy", [srcb, LZ[ri][1]], [LZ[ri][1]], out=lz5[g2 * 64:(g2 + 1) * 64, :, :, g2, :],
                     in_=s4[g2 * 64:(g2 + 1) * 64, :, :, :])
        wxs, wxsb = alloc(NGH); wxi, wxib = alloc(NGH)
        stg, stgb = alloc(KT * 2 * 128, BF16)
        stg4 = stg.rearrange("p (j r c) -> p j r c", j=KT, r=2)
        for s in range(4):
            k = 3 - s
            if k == 0:
                cur = [(BbR, BbRb), (BbI, BbIb)]
            else:
                cmul((v3(wxs), wxsb, v3(wxi), wxib), bcA(k), Bb3, three=True)
                cur = [(wxs, wxsb), (wxi, wxib)]
            for ri in range(2):
                srcw, srcwb = cur[ri]
                s3 = srcw.rearrange("p (j c) -> p j c", j=KT)
                for j in range(KT):
                    pb, pbb = bank()
                    S.op("pe", "transpose", reads=[srcwb, const_b], writes=[pbb], out=pb[0:64, 0:128], in_=s3[:, j, :],
                         identity=identF[:, :])
                    evcopy(stg4[0:64, j, ri, :], pb[0:64, 0:128], [pbb], [stgb])
            for q in range(4):
                for g2 in range(2):
                    for ri in range(2):
                        dst = dram_ap(wscr[l]["wx"], basex + (q * 32 + g2 * 16) * 4096 + s * 1024 + q * 256 + ri * 128 + g2 * 64,
                                      [[4096, 16], [128 * 4096, KT], [1, 64]])
                        S.dma("sp", dst, stg4[q * 16:(q + 1) * 16, :, ri, g2 * 64:(g2 + 1) * 64], reads=[stgb, wxz_b],
                              writes=[wreg_b[l]], arena=True)
        S.barrier()
        off[0] = mark
        CR, CRb = alloc(NGH); CI, CIb = alloc(NGH)
        cin, cinb = alloc(KT * 128)
        cin4 = cin.rearrange("p (j a c) -> p j a c", j=KT, a=2)
        for src, dst, dstb in [(c_re, CR, CRb), (c_im, CI, CIb)]:
            for q in range(4):
                for g2 in range(2):
                    srcv = dram_ap(src, l * G * HG * P_STATE + (2 * q + g2) * HG * P_STATE,
                                   [[P_STATE, HG], [8 * HG * P_STATE, KT], [1, P_STATE]])
                    S.dma("sp", cin4[q * 16:(q + 1) * 16, :, g2, :], srcv, writes=[cinb], arena=True)
            d3 = dst.rearrange("p (j c) -> p j c", j=KT)
            for j in range(KT):
                pb, pbb = bank()
                S.op("pe", "transpose", reads=[cinb, const_b], writes=[pbb], out=pb[0:128, 0:64],
                     in_=cin4[0:64, j, :, :].rearrange("p a c -> p (a c)"), identity=identF[0:64, 0:64])
                evcopy(d3[:, j, :], pb[0:128, 0:64], [pbb], [dstb])
        caR, caRb = alloc(NGH); caI, caIb = alloc(NGH)
        RZ = []
        for ri in range(2):
            rz, rzb = alloc(KT * 128, BF16)
            S.op("pool", "memset", writes=[rzb], ap=rz, constant=0.0)
            RZ.append((rz, rzb))
        wkst, wkstb = alloc(KT * 128, BF16)
        wkst3 = wkst.rearrange("p (j c) -> p j c", j=KT)
        JG = min(8, KT)
        wyst, wystb = alloc(JG * 1024, BF16)
        S.op("pool", "memset", writes=[wystb], ap=wyst, constant=0.0)
        wyst5 = wyst.rearrange("p (j q r c) -> p j q r c", j=JG, q=4, r=2)
        basek, _, _ = regs[("wk", 0)]
        basey, _, _ = regs[("wy", 0)]
        C3 = (v3(CR), CRb, v3(CI), CIb)
        for k in range(5):
            if k == 0:
                S.op("dve", "tensor_copy", [CRb], [caRb], out=caR, in_=CR)
                S.op("dve", "tensor_scalar", [CIb], [caIb], out=caI, in0=CI, scalar1=-1.0, scalar2=None, op0=ALU.mult)
            else:
                cmul((v3(caR), caRb, v3(caI), caIb), bcA(k), C3, negI=True, three=True)
            for ri, (src, srcb) in enumerate([(caR, caRb), (caI, caIb)]):
                rz5 = RZ[ri][0].rearrange("p (j q a h) -> p j q a h", j=KT, q=4, a=2)
                s4 = src.rearrange("p (j q h) -> p j q h", j=KT, q=4)
                for g2 in range(2):
                    S.op("dve", "tensor_copy", [srcb, RZ[ri][1]], [RZ[ri][1]], out=rz5[g2 * 64:(g2 + 1) * 64, :, :, g2, :],
                         in_=s4[g2 * 64:(g2 + 1) * 64, :, :, :])
            rz3 = [RZ[ri][0].rearrange("p (j c) -> p j c", j=KT) for ri in range(2)]
            lz3 = [LZ[ri][0].rearrange("p (j c) -> p j c", j=KT) for ri in range(2)]
            if k <= 3:
                for j in range(KT):
                    pb, pbb = bank()
                    S.op("pe", "matmul", reads=[LZ[0][1], RZ[0][1]], writes=[pbb], inc=False, out=pb[:, 0:128],
                         lhsT=lz3[0][:, j, :], rhs=rz3[0][:, j, :], start=True, stop=False)
                    S.op("pe", "matmul", reads=[LZ[1][1], RZ[1][1]], writes=[pbb], out=pb[:, 0:128],
                         lhsT=lz3[1][:, j, :], rhs=rz3[1][:, j, :], start=False, stop=True)
                    S.op("dve", "tensor_tensor", [pbb, const_b], [wkstb], out=wkst3[:, j, :], in0=pb[:, 0:128],
                         in1=maskbd[:], op=ALU.mult)
                dst = dram_ap(wscr[l]["wk"], basek + k * 128, [[512, 128], [128 * 512, KT], [1, 128]])
                S.dma("sp", dst, wkst3, reads=[wkstb], writes=[wreg_b[l]], arena=True)
            if k >= 1:
                t = k - 1
                for jg0 in range(0, KT, JG):
                    for ri in range(2):
                        for q in range(4):
                            S.op("pool", "tensor_copy", [RZ[ri][1], wystb], [wystb],
                                 out=wyst5[:, :, q, ri, q * 32:(q + 1) * 32],
                                 in_=rz3[ri][:, jg0:jg0 + JG, q * 32:(q + 1) * 32])
                    dst = dram_ap(wscr[l]["wy"], basey + jg0 * 128 * 4096 + t * 1024, [[4096, 128], [128 * 4096, JG], [1, 1024]])
                    S.dma("sp", dst, wyst.rearrange("p (j f) -> p j f", j=JG), reads=[wystb], writes=[wreg_b[l]],
                          arena=True)
        S.barrier()

    for l in range(DEPTH):
        s5_prep(l)

    ck2(4)
    ARENA = NUNIT * UNIT
    A_UDE = AOFF
    A_YA = AOFF + UNIT
    A_S5 = AOFF + 2 * UNIT
    SZX = 2 * WQ * NCHMAX
    SZH = WQ * NCHMAX
    assert 4 * SZX + 4 * SZH <= ARENA - 2 * UNIT, "S5 buffers do not fit"
    NET = 16
    assert UNIT + NET * 2 * NMAX <= ARENA - 2 * UNIT and 3 * UNIT <= ARENA - 2 * UNIT
    assert FT * NMAX <= ARENA - UNIT

    xF = V(XOFF, KT * NMAX, F32).rearrange("p (k n) -> p k n", k=KT)
    x_b = [Buf("x%d" % k) for k in range(KT)]
    hB = V(HOFF, KT * NMAX).rearrange("p (k n) -> p k n", k=KT)
    h_b = [Buf("h%d" % k) for k in range(KT)]

    def unit3(off):
        return V(off, KT * NMAX).rearrange("p (k n) -> p k n", k=KT)

    seq = []
    worder = weight_order(cfg)
    for ti in range(cfg.NT):
        for l in range(DEPTH):
            for (name, kc, m) in worder:
                seq.append((l, name, kc, m))
    wst = {"use": 0, "iss": 0, "done": 0}

    def w_pump():
        while wst["iss"] < len(seq) and wst["iss"] < wst["done"] + cfg.NSLOT:
            i = wst["iss"]
            l, name, kc, m = seq[i]
            base, nk, cnt = regs[(name, kc)]
            sl = i % cfg.NSLOT
            src = dram_ap(wscr[l][name], base + m * 128 * nk * 256, [[nk * 256, 128], [1, nk * 256]])
            S.dma("sp", wslots[sl][:, 0:nk * 256], src, reads=[wreg_b[l]], writes=[wslot_buf[sl]])
            wst["iss"] += 1

    def w_get(l, name, kc, m):
        i = wst["use"]
        assert seq[i] == (l, name, kc, m), (i, seq[i], (l, name, kc, m))
        assert i < wst["done"] + cfg.NSLOT, "too many live weight chunks"
        w_pump()
        wst["use"] += 1
        nk = regs[(name, kc)][1]
        return wslots[i % cfg.NSLOT], wslot_buf[i % cfg.NSLOT], nk

    def w_done(n=1):
        wst["done"] += n
        w_pump()

    def linear(l, name, m, rhs_fn, nkt, N):
        nkc = (nkt + KC - 1) // KC
        bks = [bank(), bank()]
        for kc in range(nkc):
            sl, slb, nk = w_get(l, name, kc, m)
            w3 = sl[:, 0:nk * 256].rearrange("p (k c) -> p k c", c=256)
            for mi in range(2):
                for k in range(nk):
                    kt = kc * KC + k
                    r, rb = rhs_fn(kt)
                    S.op("pe", "matmul", reads=[slb, rb], writes=[bks[mi][1]], inc=(k == nk - 1), out=bks[mi][0][:, 0:N],
                         lhsT=w3[:, k, mi * 128:(mi + 1) * 128], rhs=r, start=(kt == 0), stop=(kt == nkt - 1))
            w_done()
        return bks

    def calc_rstd(N):
        ssb, ssbb = psum[SSB], psum_b[SSB]
        S.op("act", "activation", [ssbb], [rstd_b], out=rstd[:, 0:N], in_=ssb[:, 0:N], func=AF.Sqrt, bias=EPS, scale=1.0)
        S.op("dve", "reciprocal", [rstd_b], [rstd_b], out=rstd[:, 0:N], in_=rstd[:, 0:N])

    def sumsq_acc(src_ap, src_b, first, last, N):
        sq, sqb = tb()
        S.op("act", "activation", [src_b], [sqb], out=sq[:, 0:N], in_=src_ap, func=AF.Square)
        S.op("pe", "matmul", [sqb, const_b], [psum_b[SSB]], inc=True, out=psum[SSB][:, 0:N], lhsT=onesD[:], rhs=sq[:, 0:N],
             start=first, stop=last)

    def prenorm(l, vname, N):
        for kt in range(KT):
            sumsq_acc(xF[:, kt, 0:N], x_b[kt], kt == 0, kt == KT - 1, N)
        calc_rstd(N)
        for kt in range(KT):
            eng = "dve"
            S.op(eng, "scalar_tensor_tensor", [x_b[kt], rstd_b, PVb[l]], [h_b[kt]], out=hB[:, kt, 0:N], in0=xF[:, kt, 0:N],
                 scalar=PV[l][:, VI[vname], kt:kt + 1], in1=rstd[:, 0:N], op0=ALU.mult, op1=ALU.mult)

    def residual(l, vname, src3, src_b, N):
        calc_rstd(N)
        for kt in range(KT):
            t_, t_b = tf()
            S.op("dve", "scalar_tensor_tensor", [src_b[kt], rstd_b, PVb[l]], [t_b], out=t_[:, 0:N], in0=src3[:, kt, 0:N],
                 scalar=PV[l][:, VI[vname], kt:kt + 1], in1=rstd[:, 0:N], op0=ALU.mult, op1=ALU.mult)
            S.op("pool", "tensor_tensor", [t_b, x_b[kt]], [x_b[kt]], out=xF[:, kt, 0:N], in0=xF[:, kt, 0:N], in1=t_[:, 0:N],
                 op=ALU.add)

    def hrhs(N):
        return lambda kt: (hB[:, kt, 0:N], h_b[kt])

    def mixer(l, N, segs):
        NCH = N // T
        ude_b = [Buf() for _ in range(KT)]
        yaB = unit3(A_YA); ya_b = [Buf() for _ in range(KT)]

        def ude(j):
            return V(A_UDE + j * NMAX, N).rearrange("p (s c) -> p s c", s=T)

        def udeflat(j):
            return V(A_UDE + j * NMAX, N)

        XR = [V(A_S5 + ss * 2 * SZX, WQ * NCHMAX, F32).rearrange("p (w c) -> p w c", w=WQ) for ss in range(2)]
        XI = [V(A_S5 + ss * 2 * SZX + SZX, WQ * NCHMAX, F32).rearrange("p (w c) -> p w c", w=WQ) for ss in range(2)]
        HB0 = A_S5 + 4 * SZX
        HbR = [V(HB0 + ss * 2 * SZH, WQ * NCHMAX).rearrange("p (w c) -> p w c", w=WQ) for ss in range(2)]
        HbI = [V(HB0 + ss * 2 * SZH + SZH, WQ * NCHMAX).rearrange("p (w c) -> p w c", w=WQ) for ss in range(2)]
        X_b = [[Buf(), Buf()] for _ in range(2)]
        Hb_b = [[Buf(), Buf()] for _ in range(2)]

        prenorm(l, "g_pre_mix", N)
        ck2(100)
        bks_hold = {}

        def phaseA(qq):
            ss = qq % 2
            for j in range(qq * QJ, (qq + 1) * QJ):
                if j % 2 == 0:
                    bks_hold["b"] = linear(l, "win", j // 2, hrhs(N), KT, N)
                pb, pbb = bks_hold["b"][j % 2]
                S.op("act", "copy", [pbb], [ude_b[j]], out=ude(j), in_=pb[:, 0:N].rearrange("p (c s) -> p s c", s=T))
                sl, slb, nk = w_get(l, "wx", 0, j)
                wx5 = sl[:, 0:4096].rearrange("p (s q r c) -> p s q r c", s=4, q=4, r=2)
                jl = j - qq * QJ
                for ri in range(2):
                    xb_, xbb = bank()
                    for q in range(4):
                        for s in range(4):
                            S.op("pe", "matmul", [slb, ude_b[j]], [xbb], inc=(q == 3 and s == 3),
                                 out=xb_[:, q * NCH:(q + 1) * NCH], lhsT=wx5[:, s, q, ri, :], rhs=ude(j)[:, s, :],
                                 start=(s == 0), stop=(s == 3))
                    dst = (XR if ri == 0 else XI)[ss][:, jl * 4:(jl + 1) * 4, 0:NCH]
                    evcopy(dst, xb_[:, 0:4 * NCH].rearrange("p (q c) -> p q c", q=4), [xbb], [X_b[ss][ri]], "act")
                w_done()

        def phaseB(qq):
            ss = qq % 2
            w0 = qq * WQ
            sce = "pool" if qq % 4 == 1 else "dve"
            so = 0 if sce == "pool" else 4
            sct_l = sct[so:so + 4]
            sctb_l = sct_b[so:so + 4]
            AR = A4[l][:, 0, w0:w0 + WQ]; AI = A4[l][:, 1, w0:w0 + WQ]
            XRs, XIs = XR[ss], XI[ss]
            xbR, xbI = X_b[ss]
            for seg in segs:
                cs = seg["c0"] // T; nch = seg["L"] // T
                if seg["kind"] == "p":
                    it, ib = S5car[l], S5car_b[l]
                else:
                    it, ib = S5ini[l][seg["idx"]], S5ini_b[l][seg["idx"]]
                iR = it[:, 0, w0:w0 + WQ]; iI = it[:, 1, w0:w0 + WQ]
                S.op(sce, "tensor_copy", [ib], [Hb_b[ss][0]], out=HbR[ss][:, :, cs], in_=iR)
                S.op(sce, "tensor_copy", [ib], [Hb_b[ss][1]], out=HbI[ss][:, :, cs], in_=iI)
                for ci in range(nch):
                    c = cs + ci
                    if ci == 0:
                        pR, pI, pr = iR, iI, [ib]
                    else:
                        pR, pI, pr = XRs[:, :, c - 1], XIs[:, :, c - 1], [xbR, xbI]
                    t1, t2, t3, t4 = [sct_l[i][:, :] for i in range(4)]
                    b1, b2, b3, b4 = sctb_l
                    S.op(sce, "tensor_tensor", pr + [A4b[l]], [b1], out=t1, in0=AR, in1=pR, op=ALU.mult)
                    S.op(sce, "tensor_tensor", pr + [A4b[l]], [b2], out=t2, in0=AI, in1=pI, op=ALU.mult)
                    S.op(sce, "tensor_tensor", pr + [A4b[l]], [b3], out=t3, in0=AR, in1=pI, op=ALU.mult)
                    S.op(sce, "tensor_tensor", pr + [A4b[l]], [b4], out=t4, in0=AI, in1=pR, op=ALU.mult)
                    S.op(sce, "tensor_tensor", [b1, b2], [b1], out=t1, in0=t1, in1=t2, op=ALU.subtract)
                    S.op(sce, "tensor_tensor", [b3, b4], [b3], out=t3, in0=t3, in1=t4, op=ALU.add)
                    S.op(sce, "tensor_tensor", [b1, xbR], [xbR], out=XRs[:, :, c], in0=XRs[:, :, c], in1=t1, op=ALU.add)
                    S.op(sce, "tensor_tensor", [b3, xbI], [xbI], out=XIs[:, :, c], in0=XIs[:, :, c], in1=t3, op=ALU.add)
                if nch > 1:
                    S.op("act", "copy", [xbR], [Hb_b[ss][0]], out=HbR[ss][:, :, cs + 1:cs + nch], in_=XRs[:, :, cs:cs + nch - 1])
                    S.op("act", "copy", [xbI], [Hb_b[ss][1]], out=HbI[ss][:, :, cs + 1:cs + nch], in_=XIs[:, :, cs:cs + nch - 1])
                S.op(sce, "tensor_copy", [xbR, Hb_b[ss][0]], [ib], out=iR, in_=XRs[:, :, cs + nch - 1])
                S.op(sce, "tensor_copy", [xbI, Hb_b[ss][1]], [ib], out=iI, in_=XIs[:, :, cs + nch - 1])

        def phaseC(qq):
            ss = qq % 2
            for j in range(qq * QJ, (qq + 1) * QJ):
                slk, slkb, _ = w_get(l, "wk", 0, j)
                sly, slyb, _ = w_get(l, "wy", 0, j)
                wk3 = slk[:, 0:512].rearrange("p (t c) -> p t c", t=4)
                wy5 = sly[:, 0:4096].rearrange("p (t q r c) -> p t q r c", t=4, q=4, r=2)
                yb_, ybb = bank()
                jl = j - qq * QJ
                for t in range(T):
                    mm = []
                    for s in range(t + 1):
                        mm.append((wk3[:, t - s, :], slkb, ude(j)[:, s, :], ude_b[j]))
                    for q in range(4):
                        for ri in range(2):
                            hb = (HbR if ri == 0 else HbI)[ss]
                            mm.append((wy5[:, t, q, ri, :], slyb, hb[:, jl * 4 + q, 0:NCH], Hb_b[ss][ri]))
                    for i, (lt, ltb, r, rb_) in enumerate(mm):
                        last = i == len(mm) - 1
                        S.op("pe", "matmul", [ltb, rb_], [ybb], inc=(last and t == T - 1), out=yb_[:, t * NCH:(t + 1) * NCH],
                             lhsT=lt, rhs=r, start=(i == 0), stop=last)
                w_done(2)
                tv, tvb = tf()
                S.op("dve", "scalar_tensor_tensor", [ude_b[j], ybb, PVb[l]], [tvb], out=tv[:, 0:N], in0=udeflat(j),
                     scalar=PV[l][:, VI["s5_d"], j:j + 1], in1=yb_[:, 0:N], op0=ALU.mult, op1=ALU.add)
                S.op("act", "activation", [tvb], [ya_b[j]], out=yaB[:, j, 0:N].rearrange("p (c s) -> p s c", s=T),
                     in_=tv[:, 0:N].rearrange("p (s c) -> p s c", s=T), func=AF.Gelu_apprx_tanh)

        phaseA(0)
        ck2(101)
        for qq in range(1, NQ):
            phaseA(qq); phaseB(qq - 1); phaseC(qq - 1)
        ck2(102)
        phaseB(NQ - 1); phaseC(NQ - 1)
        S.barrier()
        ck2(103)

        ya2B = unit3(A_UDE); ya2_b = [Buf() for _ in range(KT)]
        for m in range(M2):
            bks = linear(l, "glu", m, lambda kt: (yaB[:, kt, 0:N], ya_b[kt]), KT, N)
            for mi in range(2):
                kt = 2 * m + mi
                sg, sgb = tb()
                S.op("act", "activation", [bks[mi][1], PVb[l]], [sgb], out=sg[:, 0:N], in_=bks[mi][0][:, 0:N],
                     func=AF.Sigmoid, bias=PV[l][:, VI["b_glu"], kt:kt + 1], scale=1.0)
                S.op("pool", "tensor_tensor", [sgb, ya_b[kt]], [ya2_b[kt]], out=ya2B[:, kt, 0:N], in0=yaB[:, kt, 0:N],
                     in1=sg[:, 0:N], op=ALU.mult)

        ck2(104)
        ybB = unit3(A_S5); yb_b = [Buf() for _ in range(KT)]
        ET = [V(A_S5 + UNIT + i * 2 * NMAX, NMAX, F32) for i in range(NET)]
        ET_b = [Buf() for _ in range(NET)]
        for b in range(NB):
            bx = linear(l, "win", M2 + b, hrhs(N), KT, N)
            xcbf = []
            for mi in range(2):
                kt = 2 * b + mi
                xp, xpb = xpad[mi], xpad_b[mi]
                xc, xcb = ET[mi], ET_b[mi]
                pb, pbb = bx[mi]
                pv = PV[l]
                for si, seg in enumerate(segs):
                    c0, L = seg["c0"], seg["L"]
                    off = c0 + 3 * si
                    cidx = 0 if seg["kind"] == "p" else 1 + seg["idx"]
                    S.op("pool", "tensor_copy", [CONVcar_b[l]], [xpb], out=xp[:, off:off + 3], in_=CONVcar[l][:, cidx, :, kt])
                    S.op("act", "copy", [pbb], [xpb], out=xp[:, off + 3:off + 3 + L], in_=pb[:, c0:c0 + L])
                    S.op("pool", "tensor_copy", [xpb], [CONVcar_b[l]], out=CONVcar[l][:, cidx, :, kt], in_=xp[:, off + L:off + L + 3])
                    S.op("dve", "tensor_scalar", [xpb, PVb[l]], [xcb], out=xc[:, c0:c0 + L], in0=xp[:, off + 3:off + 3 + L],
                         scalar1=pv[:, VI["conv_w3"], kt:kt + 1], scalar2=pv[:, VI["conv_b"], kt:kt + 1], op0=ALU.mult,
                         op1=ALU.add)
                    for k in range(3):
                        S.op("dve", "scalar_tensor_tensor", [xpb, xcb, PVb[l]], [xcb], out=xc[:, c0:c0 + L],
                             in0=xp[:, off + k:off + k + L], scalar=pv[:, VI["conv_w%d" % k], kt:kt + 1], in1=xc[:, c0:c0 + L],
                             op0=ALU.mult, op1=ALU.add)
                xh, xhb = tb()
                S.op("pool", "tensor_copy", [xcb], [xhb], out=xh[:, 0:N], in_=xc[:, 0:N])
                xcbf.append((xh, xhb))
            slr, slrb, _ = w_get(l, "wr", 0, b)
            sli, slib, _ = w_get(l, "wi", 0, b)
            wr3 = slr[:, 0:512].rearrange("p (k c) -> p k c", k=2)
            wi3 = sli[:, 0:512].rearrange("p (k c) -> p k c", k=2)
            gate_banks = []
            for mo in range(2):
                rb_, rbb = bank(); ib_, ibb = bank()
                for (w3, wb_, ob, obb) in [(wr3, slrb, rb_, rbb), (wi3, slib, ib_, ibb)]:
                    for k2 in range(2):
                        S.op("pe", "matmul", [wb_, xcbf[k2][1]], [obb], inc=(k2 == 1), out=ob[:, 0:N],
                             lhsT=w3[:, k2, mo * 128:(mo + 1) * 128], rhs=xcbf[k2][0][:, 0:N], start=(k2 == 0), stop=(k2 == 1))
                gate_banks.append((rb_, rbb, ib_, ibb))
            w_done(2)
            hs_list = []

            def ets(mo):
                e0 = 2 + mo * 5
                return (ET[e0], ET_b[e0], ET[e0 + 1], ET_b[e0 + 1], ET[e0 + 2], ET_b[e0 + 2], ET[e0 + 3], ET_b[e0 + 3])
            for mo in range(2):
                kt = 2 * b + mo
                rb_, rbb, ib_, ibb = gate_banks[mo]
                rT, rTb, iT, iTb, a2T, a2Tb, hsT, hsTb = ets(mo)
                S.op("act", "activation", [rbb, PVb[l]], [rTb], out=rT[:, 0:N], in_=rb_[:, 0:N], func=AF.Sigmoid,
                     bias=pv[:, VI["lru_b_r"], kt:kt + 1], scale=1.0)
                S.op("act", "activation", [ibb, PVb[l]], [iTb], out=iT[:, 0:N], in_=ib_[:, 0:N], func=AF.Sigmoid,
                     bias=pv[:, VI["lru_b_i"], kt:kt + 1], scale=1.0)
            for mo in range(2):
                kt = 2 * b + mo
                rT, rTb, iT, iTb, a2T, a2Tb, hsT, hsTb = ets(mo)
                S.op("act", "activation", [rTb, PVb[l]], [a2Tb], out=a2T[:, 0:N], in_=rT[:, 0:N], func=AF.Exp,
                     scale=C8[l][:, 1, kt:kt + 1])
                S.op("act", "activation", [rTb, PVb[l]], [rTb], out=rT[:, 0:N], in_=rT[:, 0:N], func=AF.Exp,
                     scale=C8[l][:, 0, kt:kt + 1])
            for mo in range(2):
                rT, rTb, iT, iTb, a2T, a2Tb, hsT, hsTb = ets(mo)
                S.op("act", "activation", [a2Tb], [a2Tb], out=a2T[:, 0:N], in_=a2T[:, 0:N], func=AF.Sqrt, bias=1.0, scale=-1.0)
            for mo in range(2):
                kt = 2 * b + mo
                rT, rTb, iT, iTb, a2T, a2Tb, hsT, hsTb = ets(mo)
                xc, xcb = ET[mo], ET_b[mo]
                S.op("dve", "tensor_tensor", [iTb, xcb], [iTb], out=iT[:, 0:N], in0=iT[:, 0:N], in1=xc[:, 0:N], op=ALU.mult)
                S.op("dve", "tensor_tensor", [iTb, a2Tb], [iTb], out=iT[:, 0:N], in0=iT[:, 0:N], in1=a2T[:, 0:N], op=ALU.mult)
                for seg in segs:
                    c0, L = seg["c0"], seg["L"]
                    cidx = 0 if seg["kind"] == "p" else 1 + seg["idx"]
                    S.op("dve", "tensor_tensor_scan", [rTb, iTb, LRUcar_b[l]], [hsTb], out=hsT[:, c0:c0 + L],
                         data0=rT[:, c0:c0 + L], data1=iT[:, c0:c0 + L], initial=LRUcar[l][:, cidx, kt:kt + 1],
                         op0=ALU.mult, op1=ALU.add)
                    S.op("pool", "tensor_copy", [hsTb], [LRUcar_b[l]], out=LRUcar[l][:, cidx, kt:kt + 1],
                         in_=hsT[:, c0 + L - 1:c0 + L])
                hs_list.append((hsT, hsTb))
            bg = linear(l, "win", 2 * M2 + b, hrhs(N), KT, N)
            for mi in range(2):
                kt = 2 * b + mi
                gg, ggb = ET[12 + mi], ET_b[12 + mi]
                S.op("act", "activation", [bg[mi][1]], [ggb], out=gg[:, 0:N], in_=bg[mi][0][:, 0:N], func=AF.Gelu_apprx_tanh)
                S.op("dve", "tensor_tensor", [ggb, hs_list[mi][1]], [yb_b[kt]], out=ybB[:, kt, 0:N], in0=hs_list[mi][0][:, 0:N],
                     in1=gg[:, 0:N], op=ALU.mult)
        S.barrier()

        ck2(105)
        mgB = unit3(A_S5 + UNIT); mg_b = [Buf() for _ in range(KT)]
        for m in range(M2):
            ga = linear(l, "win", 3 * M2 + m, hrhs(N), KT, N)
            gb = linear(l, "win", 4 * M2 + m, hrhs(N), KT, N)
            sgs = []
            for mi in range(2):
                sa, sab = tf(); sb_, sbb = tf()
                S.op("act", "activation", [ga[mi][1]], [sab], out=sa[:, 0:N], in_=ga[mi][0][:, 0:N], func=AF.Sigmoid)
                S.op("act", "activation", [gb[mi][1]], [sbb], out=sb_[:, 0:N], in_=gb[mi][0][:, 0:N], func=AF.Sigmoid)
                sgs.append((sa, sab, sb_, sbb))
            pa_ = linear(l, "pa", m, lambda kt: (ya2B[:, kt, 0:N], ya2_b[kt]), KT, N)
            pb_ = linear(l, "pb", m, lambda kt: (ybB[:, kt, 0:N], yb_b[kt]), KT, N)
            for mi in range(2):
                kt = 2 * m + mi
                sa, sab, sb_, sbb = sgs[mi]
                S.op("dve", "tensor_tensor", [sab, pa_[mi][1]], [sab], out=sa[:, 0:N], in0=sa[:, 0:N], in1=pa_[mi][0][:, 0:N],
                     op=ALU.mult)
                S.op("dve", "tensor_tensor", [sbb, pb_[mi][1]], [sbb], out=sb_[:, 0:N], in0=sb_[:, 0:N], in1=pb_[mi][0][:, 0:N],
                     op=ALU.mult)
                S.op("pool", "tensor_tensor", [sab, sbb], [mg_b[kt]], out=mgB[:, kt, 0:N], in0=sa[:, 0:N], in1=sb_[:, 0:N],
                     op=ALU.add)

        ck2(106)
        oB = unit3(A_S5 + 2 * UNIT); o_b = [Buf() for _ in range(KT)]
        for m in range(M2):
            ob = linear(l, "wout", m, lambda kt: (mgB[:, kt, 0:N], mg_b[kt]), KT, N)
            for mi in range(2):
                kt = 2 * m + mi
                evcopy(oB[:, kt, 0:N], ob[mi][0][:, 0:N], [ob[mi][1]], [o_b[kt]])
        for kt in range(KT):
            sumsq_acc(oB[:, kt, 0:N], o_b[kt], kt == 0, kt == KT - 1, N)
        ck2(107)
        residual(l, "g_post_mix", oB, o_b, N)
        ck2(108)
        S.barrier()

    def ffn(l, N):
        actB = V(AOFF, FT * NMAX).rearrange("p (k n) -> p k n", k=FT)
        act_b = [Buf() for _ in range(FT)]
        fB = unit3(AOFF + ARENA - UNIT); f_b = [Buf() for _ in range(KT)]
        prenorm(l, "g_pre_ffn", N)
        for f in range(F2):
            gt = linear(l, "gate", f, hrhs(N), KT, N)
            up = linear(l, "up", f, hrhs(N), KT, N)
            for mi in range(2):
                ft = 2 * f + mi
                sg, sgb = tf()
                S.op("act", "activation", [gt[mi][1]], [sgb], out=sg[:, 0:N], in_=gt[mi][0][:, 0:N], func=AF.Silu)
                S.op("dve", "tensor_tensor", [sgb, up[mi][1]], [act_b[ft]], out=actB[:, ft, 0:N], in0=sg[:, 0:N],
                     in1=up[mi][0][:, 0:N], op=ALU.mult)
        for m in range(M2):
            ob = linear(l, "down", m, lambda ft: (actB[:, ft, 0:N], act_b[ft]), FT, N)
            for mi in range(2):
                kt = 2 * m + mi
                evcopy(fB[:, kt, 0:N], ob[mi][0][:, 0:N], [ob[mi][1]], [f_b[kt]])
        for kt in range(KT):
            sumsq_acc(fB[:, kt, 0:N], f_b[kt], kt == 0, kt == KT - 1, N)
        residual(l, "g_post_ffn", fB, f_b, N)
        S.barrier()

    for ti in range(cfg.NT):
        last_tile = ti == cfg.NT - 1
        N = NMAX if last_tile else NP
        segs = [dict(kind="p", idx=0, c0=0, L=NP)]
        blocks = [(x_p, y_p, ti * NP + r * 128, 128, r * 128) for r in range(NP // 128)]
        if last_tile:
            for s in range(NS):
                segs.append(dict(kind="s", idx=s, c0=NP + s * DEC, L=DEC))
            blocks.append((x_s, y_s, 0, NSAMP, NP))
        xin = []
        for bi, (src, _, r0, rows, c0) in enumerate(blocks):
            st_ = V(AOFF + bi * 2 * D, D, F32); stb_ = Buf()
            S.dma("sp", st_[0:rows, :], src[r0:r0 + rows, :], writes=[stb_], arena=True)
            xin.append((st_, stb_))
        for kt in range(KT):
            pb, pbb = bank()
            for bi, (src, _, r0, rows, c0) in enumerate(blocks):
                S.op("pe", "transpose", [xin[bi][1], const_b], [pbb], inc=(bi == len(blocks) - 1), out=pb[:, c0:c0 + rows],
                     in_=xin[bi][0][0:rows, kt * 128:(kt + 1) * 128], identity=identF[0:rows, 0:rows])
            evcopy(xF[:, kt, 0:N], pb[:, 0:N], [pbb], [x_b[kt]])
        S.barrier()
        ck2(5 + 10 * ti)
        for l in range(DEPTH):
            mixer(l, N, segs)
            ck2(6 + 10 * ti + 2 * l)
            ffn(l, N)
            ck2(7 + 10 * ti + 2 * l)
        for bi, (_, dst, r0, rows, c0) in enumerate(blocks):
            st_ = V(AOFF + bi * 2 * D, D, F32); stb_ = Buf()
            for k4 in range(0, KT, 4):
                nk4 = min(4, KT - k4)
                pb, pbb = bank()
                for k in range(nk4):
                    S.op("pe", "transpose", [x_b[k4 + k], const_b], [pbb], inc=(k == nk4 - 1),
                         out=pb[0:rows, k * 128:(k + 1) * 128], in_=xF[:, k4 + k, c0:c0 + rows], identity=identF[:, :])
                evcopy(st_[0:rows, k4 * 128:(k4 + nk4) * 128], pb[0:rows, 0:nk4 * 128], [pbb], [stb_])
            S.dma("sp", dst[r0:r0 + rows, :], st_[0:rows, :], reads=[stb_], arena=True)
        S.barrier()

    for l in range(DEPTH):
        for ri, dst in enumerate([o_ps5re, o_ps5im]):
            transpose_to(iost[0:GQ, :], S5car[l][:, ri, :], 128, GQ, [S5car_b[l]], [iost_b])
            S.dma("sp", dst[l], iost[0:GQ, :], reads=[iost_b])
        transpose_to(iost[0:KT, :], LRUcar[l][:, 0, :], 128, KT, [LRUcar_b[l]], [iost_b])
        S.dma("sp", o_plru[l], iost[0:KT, :], reads=[iost_b])
        transpose_to(iost[0:3 * KT, :], CONVcar[l][:, 0, :, :].rearrange("p a k -> p (a k)"), 128, 3 * KT, [CONVcar_b[l]],
                     [iost_b])
        S.dma("sp", o_pconv[l], iost[0:3 * KT, :], reads=[iost_b])
        for s in range(NS):
            for ri, dst in enumerate([o_ss5re, o_ss5im]):
                transpose_to(iost[0:GQ, :], S5ini[l][s][:, ri, :], 128, GQ, [S5ini_b[l][s]], [iost_b])
                S.dma("sp", dst[l, s], iost[0:GQ, :], reads=[iost_b])
            transpose_to(iost[0:KT, :], LRUcar[l][:, 1 + s, :], 128, KT, [LRUcar_b[l]], [iost_b])
            S.dma("sp", o_slru[l, s], iost[0:KT, :], reads=[iost_b])
            transpose_to(iost[0:3 * KT, :], CONVcar[l][:, 1 + s, :, :].rearrange("p a k -> p (a k)"), 128, 3 * KT,
                         [CONVcar_b[l]], [iost_b])
            S.dma("sp", o_sconv[l, s], iost[0:3 * KT, :], reads=[iost_b])
    assert wst["use"] == len(seq) and wst["done"] == len(seq)
    finalize()


def make_in_maps(cfg, inp):
    n_cores = 8
    D, KT, G, GQ, NS, DEPTH = cfg.D, cfg.KT, cfg.G, cfg.GQ, cfg.NS, cfg.DEPTH
    f = lambda a: np.ascontiguousarray(np.asarray(a, dtype=np.float32))
    shared = {}
    for n in ["g_pre_mix", "s5_d", "b_glu", "conv_b", "lru_b_r", "lru_b_i", "lru_lam", "g_post_mix", "g_pre_ffn",
              "g_post_ffn"]:
        shared[n] = f(inp[n]).reshape(DEPTH, KT, 128)
    shared["conv_w"] = f(inp["conv_w"]).reshape(DEPTH, 4, KT, 128)
    for n in ["w_in", "w_glu", "p_a", "p_b", "w_out", "lru_w_r", "lru_w_i", "w_gate", "w_up", "w_down", "s5_b_re",
              "s5_b_im", "s5_c_re", "s5_c_im"]:
        shared[n] = f(inp[n])
    shared["s5_lam_re"] = f(inp["s5_lam_re"]).reshape(DEPTH, GQ, 128)
    shared["s5_lam_im"] = f(inp["s5_lam_im"]).reshape(DEPTH, GQ, 128)
    shared["s5_log_step"] = f(inp["s5_log_step"]).reshape(DEPTH, GQ, 2)
    maps = []
    for c in range(n_cores):
        m = dict(shared)
        m["x_p"] = f(inp["x_prompt"][c])
        m["x_s"] = f(inp["x_sample"][NS * c:NS * (c + 1)]).reshape(cfg.NSAMP, D)
        m["i_s5re"] = f(inp["state_s5_re"][:, NS * c:NS * (c + 1)]).reshape(DEPTH, NS, GQ, 128)
        m["i_s5im"] = f(inp["state_s5_im"][:, NS * c:NS * (c + 1)]).reshape(DEPTH, NS, GQ, 128)
        m["i_lru"] = f(inp["state_lru"][:, NS * c:NS * (c + 1)]).reshape(DEPTH, NS, KT, 128)
        m["i_conv"] = f(inp["cache_conv"][:, NS * c:NS * (c + 1)]).reshape(DEPTH, NS, 3 * KT, 128)
        maps.append(m)
    return maps


def gather(cfg, results):
    D, G, NS, DEPTH, SEQ, DEC = cfg.D, cfg.G, cfg.NS, cfg.DEPTH, cfg.SEQ, cfg.DEC_SEQ
    nc_ = len(results)
    yp = np.stack([r["y_p"] for r in results]).reshape(nc_, SEQ, D)
    ys = np.concatenate([r["y_s"].reshape(NS, DEC, D) for r in results], axis=0)
    pre = np.stack([r["o_ps5re"].reshape(DEPTH, G, P_STATE) for r in results], axis=1)
    pim = np.stack([r["o_ps5im"].reshape(DEPTH, G, P_STATE) for r in results], axis=1)
    plru = np.stack([r["o_plru"].reshape(DEPTH, D) for r in results], axis=1)
    pconv = np.stack([r["o_pconv"].reshape(DEPTH, 3, D) for r in results], axis=1)
    sre = np.concatenate([r["o_ss5re"].reshape(DEPTH, NS, G, P_STATE) for r in results], axis=1)
    sim = np.concatenate([r["o_ss5im"].reshape(DEPTH, NS, G, P_STATE) for r in results], axis=1)
    slru = np.concatenate([r["o_slru"].reshape(DEPTH, NS, D) for r in results], axis=1)
    sconv = np.concatenate([r["o_sconv"].reshape(DEPTH, NS, 3, D) for r in results], axis=1)
    outs = (yp, ys, pre, pim, plru, pconv, sre, sim, slru, sconv)
    return tuple(np.ascontiguousarray(o.astype(np.float32)) for o in outs)


def run(cfg, inp):
    nc, _ = build(cfg)
    maps = make_in_maps(cfg, inp)
    res = run_bass_kernel_spmd(nc, maps, core_ids=list(range(8)))
    return gather(cfg, res.results)


def kernel(**inputs):
    return run(Cfg(), inputs)
```
